# Optimizing a Trainium2 kernel written in Bass

```python
import jax, jax.numpy as jnp
from jax import lax
import numpy as np

D_MODEL = 4096
BATCH = 1
SEQ = 16384
DEPTH = 4

CHUNK = 64
N_EVEN = (DEPTH + 1) // 2
N_ODD = DEPTH // 2

MIX_WIDTH = D_MODEL
D_A = MIX_WIDTH // 2
D_B = MIX_WIDTH // 2
A_GROUPS = 16
B_GROUPS = 16
A_CONV_W = 3
B_CONV_W = 31
P_IN = 3 * D_A + 2 * D_B

POOL_WINDOWS = (2, 4, 8, 16)
N_POOL_GROUPS = len(POOL_WINDOWS)
POOL_GROUP = D_MODEL // N_POOL_GROUPS

D_FF = 11008
FFN_CONV_W = 3
EPS = 1e-6

kernel_name = "hybrid_conv_pool_streaming_encoder"


def rms_norm(x, g):
    xf = x.astype(jnp.float32)
    y = xf * lax.rsqrt(jnp.mean(xf * xf, axis=-1, keepdims=True) + EPS)
    return (y * g.astype(jnp.float32)).astype(x.dtype)


def layer_norm(x, g, b):
    xf = x.astype(jnp.float32)
    mu = jnp.mean(xf, axis=-1, keepdims=True)
    xc = xf - mu
    y = xc * lax.rsqrt(jnp.mean(xc * xc, axis=-1, keepdims=True) + EPS)
    return (y * g.astype(jnp.float32) + b.astype(jnp.float32)).astype(x.dtype)


def causal_dwconv(x, w):
    k, c = w.shape
    return lax.conv_general_dilated(
        x, w[:, None, :].astype(x.dtype),
        window_strides=(1,), padding=((k - 1, 0),),
        dimension_numbers=("NWC", "WIO", "NWC"),
        feature_group_count=c)


def hybrid_conv_mixer(h, w_in, conv_a, conv_b, conv_b_bias, ln_g, ln_b, w_out):
    p = h @ w_in
    ax, ac, ab, bv, bg = jnp.split(
        p, [D_A, 2 * D_A, 3 * D_A, 3 * D_A + D_B], axis=-1)
    y_a = ab * causal_dwconv(ac * ax, conv_a)
    u = bv * jax.nn.sigmoid(bg)
    u = causal_dwconv(u, conv_b) + conv_b_bias
    y_b = jax.nn.silu(layer_norm(u, ln_g, ln_b))
    return jnp.concatenate([y_a, y_b], axis=-1) @ w_out


def multiscale_pool_mixer(h, w_groups, scale):
    b, s, d = h.shape
    hg = h.reshape(b, s, N_POOL_GROUPS, POOL_GROUP)
    cs = jnp.cumsum(hg.astype(jnp.float32), axis=1)
    pos = jnp.arange(s)
    diffs = []
    for g, win in enumerate(POOL_WINDOWS):
        c = cs[:, :, g]
        lagged = jnp.pad(c, ((0, 0), (win, 0), (0, 0)))[:, :s]
        cnt = jnp.minimum(pos + 1, win).astype(jnp.float32)[None, :, None]
        diffs.append((c - lagged) / cnt - hg[:, :, g].astype(jnp.float32))
    dpool = jnp.stack(diffs, axis=2).astype(h.dtype)
    y = jnp.einsum("bsgc,gcd->bsgd", dpool, w_groups).reshape(b, s, d)
    return y * scale


def conv_ffn(h, w_up, conv, conv_bias, w_down):
    u = causal_dwconv(h @ w_up, conv) + conv_bias
    gate, up = jnp.split(u, 2, axis=-1)
    return (jax.nn.silu(gate) * up) @ w_down


def setup_inputs(seed: int = 0) -> dict:
    key = jax.random.key(seed)
    ks = jax.random.split(key, 18)
    n = jax.random.normal
    f32 = jnp.float32
    return {
        "x": n(ks[0], (BATCH, SEQ, D_MODEL), f32),
        "mix_norm": 1.0 + 0.05 * n(ks[1], (DEPTH, D_MODEL), f32),
        "ffn_norm": 1.0 + 0.05 * n(ks[2], (DEPTH, D_MODEL), f32),
        "final_norm": 1.0 + 0.05 * n(ks[3], (D_MODEL,), f32),
        "hyb_w_in": n(ks[4], (N_EVEN, D_MODEL, P_IN), f32) * D_MODEL ** -0.5,
        "hyb_conv_a": n(ks[5], (N_EVEN, A_CONV_W, D_A), f32) * A_CONV_W ** -0.5,
        "hyb_conv_b": n(ks[6], (N_EVEN, B_CONV_W, D_B), f32) * B_CONV_W ** -0.5,
        "hyb_conv_b_bias": 0.02 * n(ks[7], (N_EVEN, D_B), f32),
        "hyb_ln_g": 1.0 + 0.05 * n(ks[8], (N_EVEN, D_B), f32),
        "hyb_ln_b": 0.02 * n(ks[9], (N_EVEN, D_B), f32),
        "hyb_w_out": n(ks[10], (N_EVEN, MIX_WIDTH, D_MODEL), f32) * MIX_WIDTH ** -0.5,
        "pool_w": n(ks[11], (N_ODD, N_POOL_GROUPS, POOL_GROUP, POOL_GROUP), f32) * POOL_GROUP ** -0.5,
        "pool_scale": 0.5 + 0.1 * n(ks[12], (N_ODD, D_MODEL), f32),
        "ffn_w_up": n(ks[13], (DEPTH, D_MODEL, 2 * D_FF), f32) * D_MODEL ** -0.5,
        "ffn_conv": n(ks[14], (DEPTH, FFN_CONV_W, 2 * D_FF), f32) * FFN_CONV_W ** -0.5,
        "ffn_conv_bias": 0.02 * n(ks[15], (DEPTH, 2 * D_FF), f32),
        "ffn_w_down": n(ks[16], (DEPTH, D_FF, D_MODEL), f32) * D_FF ** -0.5,
    }


def reference(x, mix_norm, ffn_norm, final_norm, hyb_w_in, hyb_conv_a, hyb_conv_b,
              hyb_conv_b_bias, hyb_ln_g, hyb_ln_b, hyb_w_out, pool_w, pool_scale,
              ffn_w_up, ffn_conv, ffn_conv_bias, ffn_w_down):
    for l in range(DEPTH):
        h = rms_norm(x, mix_norm[l])
        if l % 2 == 0:
            i = l // 2
            y = hybrid_conv_mixer(h, hyb_w_in[i], hyb_conv_a[i], hyb_conv_b[i],
                                  hyb_conv_b_bias[i], hyb_ln_g[i], hyb_ln_b[i],
                                  hyb_w_out[i])
        else:
            i = l // 2
            y = multiscale_pool_mixer(h, pool_w[i], pool_scale[i])
        x = x + y
        h = rms_norm(x, ffn_norm[l])
        x = x + conv_ffn(h, ffn_w_up[l], ffn_conv[l], ffn_conv_bias[l], ffn_w_down[l])
    return rms_norm(x, final_norm)
```

```python
import contextlib
import numpy as np
import concourse.bass as bass
import concourse.mybir as mybir
from concourse.bass_utils import run_bass_kernel_spmd

F32 = mybir.dt.float32
BF16 = mybir.dt.bfloat16
AF = mybir.ActivationFunctionType
ALU = mybir.AluOpType
ET = mybir.EngineType

EPS = 1e-6
POOL_WINDOWS = (2, 4, 8, 16)
A_CONV_W = 3
B_CONV_W = 31
FFN_CONV_W = 3
GSZ = 4
NSLOT = 6
NTMP = 7
HIST = 32


class Cfg:
    def __init__(self, D, DA, DB, DFF, DEPTH, S, T, NCORES=1):
        self.D, self.DA, self.DB, self.DFF, self.DEPTH, self.S, self.T = D, DA, DB, DFF, DEPTH, S, T
        self.NCORES = NCORES
        self.KC, self.CA, self.CB, self.CF = D // 128, DA // 128, DB // 128, DFF // 128
        assert D % 512 == 0 and DA % 128 == 0 and DB % 128 == 0 and DFF % 128 == 0 and DA + DB == D
        assert S % NCORES == 0
        self.SC = S // NCORES
        self.H = 0 if NCORES == 1 else sum((B_CONV_W - 1 if l % 2 == 0 else POOL_WINDOWS[-1] - 1)
                                           + FFN_CONV_W - 1 for l in range(DEPTH))
        assert self.H + 16 <= T
        self.NT = -(-(self.H + self.SC) // T)
        self.SP = self.NT * T
        self.NE = (DEPTH + 1) // 2
        self.NO = DEPTH // 2
        self.PK = self.KC // 4


def unit_plan(cfg):
    units = []
    for l in range(cfg.DEPTH):
        i = l // 2
        if l % 2 == 0:
            for c in range(max(cfg.CA, cfg.CB)):
                if c < cfg.CB:
                    units.append(("bv", i, c, 0, cfg.KC))
                    units.append(("bg", i, c, 0, cfg.KC))
                if c < cfg.CA:
                    units.append(("ax", i, c, 0, cfg.KC))
                    units.append(("ac", i, c, 0, cfg.KC))
                    units.append(("ab", i, c, 0, cfg.KC))
            for d in range(cfg.KC):
                units.append(("wo", i, d, 0, cfg.KC))
        else:
            for g in range(4):
                for dl in range(cfg.PK):
                    units.append(("pl", i, g, dl, cfg.PK))
        for j0 in range(0, cfg.CF, GSZ):
            grp = list(range(j0, min(j0 + GSZ, cfg.CF)))
            for j in grp:
                units.append(("fg", l, j, 0, cfg.KC))
                units.append(("fu", l, j, 0, cfg.KC))
            for j in grp:
                units.append(("fd", l, j, 0, cfg.KC))
    return units


def _colunit(W, f0):
    k = W.shape[0] // 128
    return W[:, f0:f0 + 128].reshape(k, 128, 128).transpose(1, 0, 2)


def pack_weights(cfg, inp):
    units = unit_plan(cfg)
    rows = [0]
    for u in units:
        rows.append(rows[-1] + 128 * u[4])
    nrow = rows[-1]
    CVB = 128 * 64
    nrow_pad = -(-nrow // CVB) * CVB
    wall = np.zeros((nrow_pad, 128), np.float32)
    DA, DB, DFF = cfg.DA, cfg.DB, cfg.DFF
    for u, r0 in zip(units, rows[:-1]):
        kind, li, a, b, nk = u
        if kind in ("ax", "ac", "ab", "bv", "bg"):
            base = {"ax": 0, "ac": DA, "ab": 2 * DA, "bv": 3 * DA, "bg": 3 * DA + DB}[kind]
            blk = _colunit(inp["hyb_w_in"][li], base + a * 128)
        elif kind == "wo":
            blk = _colunit(inp["hyb_w_out"][li], a * 128)
        elif kind == "pl":
            blk = _colunit(inp["pool_w"][li][a], b * 128)
        elif kind == "fg":
            blk = _colunit(inp["ffn_w_up"][li], a * 128)
        elif kind == "fu":
            blk = _colunit(inp["ffn_w_up"][li], DFF + a * 128)
        elif kind == "fd":
            blk = inp["ffn_w_down"][li][a * 128:(a + 1) * 128, :].reshape(128, cfg.KC, 128)
        wall[r0:r0 + 128 * nk, :] = blk.reshape(128 * nk, 128)
    return wall, rows


class ParamPack:
    def __init__(self):
        self.cols = []
        self.off = {}
        self.n = 0

    def add(self, name, arr):
        arr = np.ascontiguousarray(arr, dtype=np.float32).reshape(128, -1)
        self.off[name] = self.n
        self.n += arr.shape[1]
        self.cols.append(arr)

    def build(self):
        return np.concatenate(self.cols, axis=1)


def _vec(v):
    return np.asarray(v).reshape(-1, 128).T


def _taps(w):
    K, C = w.shape
    return np.asarray(w).reshape(K, C // 128, 128).transpose(2, 0, 1).reshape(128, -1)


def pack_params(cfg, inp, with_data=True):
    pk = ParamPack()
    z = lambda *s: np.zeros(s, np.float32)
    for l in range(cfg.DEPTH):
        pk.add(f"mixn{l}", _vec(inp["mix_norm"][l]) if with_data else z(128, cfg.KC))
        pk.add(f"ffnn{l}", _vec(inp["ffn_norm"][l]) if with_data else z(128, cfg.KC))
        pk.add(f"fcw{l}", _taps(inp["ffn_conv"][l]) if with_data else z(128, 3 * 2 * cfg.CF))
        pk.add(f"fcb{l}", _vec(inp["ffn_conv_bias"][l]) if with_data else z(128, 2 * cfg.CF))
    pk.add("finn", _vec(inp["final_norm"]) if with_data else z(128, cfg.KC))
    for i in range(cfg.NE):
        pk.add(f"caw{i}", _taps(inp["hyb_conv_a"][i]) if with_data else z(128, A_CONV_W * cfg.CA))
        pk.add(f"cbw{i}", _taps(inp["hyb_conv_b"][i]) if with_data else z(128, B_CONV_W * cfg.CB))
        pk.add(f"cbb{i}", _vec(inp["hyb_conv_b_bias"][i]) if with_data else z(128, cfg.CB))
        pk.add(f"lng{i}", _vec(inp["hyb_ln_g"][i]) if with_data else z(128, cfg.CB))
        pk.add(f"lnb{i}", _vec(inp["hyb_ln_b"][i]) if with_data else z(128, cfg.CB))
    for i in range(cfg.NO):
        pk.add(f"psc{i}", _vec(inp["pool_scale"][i]) if with_data else z(128, cfg.KC))
    pk.add("pos16", np.tile(np.arange(1, 17, dtype=np.float32)[None, :], (128, 1)))
    return pk


class Buf:
    __slots__ = ("w", "r")

    def __init__(self):
        self.w = None
        self.r = {}


class Sync:
    def __init__(self, nc, stack):
        self.nc = nc
        self.eng = {"pe": nc.tensor, "act": nc.scalar, "dve": nc.vector, "sp": nc.sync, "pool": nc.gpsimd}
        self.semh = {}
        self.cnt = {}
        for k in self.eng:
            self.semh[k] = stack.enter_context(nc.semaphore("i_" + k))
            self.cnt[k] = 0
        self.stack = stack
        self.seen = {k: {} for k in self.eng}
        self.bufs = []

    def buf(self):
        b = Buf()
        self.bufs.append(b)
        return b

    def dma_sem(self, name):
        self.semh[name] = self.stack.enter_context(self.nc.semaphore("d_" + name))
        self.cnt[name] = 0
        return name

    def _need(self, e, reads, writes):
        ev = []
        for b in reads:
            if b.w is not None:
                ev.append(b.w)
        for b in writes:
            if b.w is not None and b.w[2] != e:
                ev.append(b.w)
            for r in b.r.values():
                if r[2] != e:
                    ev.append(r)
        return ev

    def _wait(self, e, events):
        best = {}
        for (sk, v, _pe) in events:
            if v > best.get(sk, 0):
                best[sk] = v
        for sk, v in best.items():
            if self.seen[e].get(sk, 0) >= v:
                continue
            self.eng[e].wait_ge(self.semh[sk], v)
            self.seen[e][sk] = v

    def _record(self, ev, ekey, reads, writes):
        for b in reads:
            b.r[ekey] = ev
        for b in writes:
            b.w = ev
            b.r = {}

    def op(self, e, emit, reads=(), writes=()):
        self._wait(e, self._need(e, reads, writes))
        ins = emit()
        self.cnt[e] += 1
        ins.then_inc(self.semh[e], 1)
        self._record((e, self.cnt[e], e), e, reads, writes)

    def dma(self, q, sem, out, in_, reads=(), writes=()):
        self._wait(q, self._need(None, reads, writes))
        self.cnt[sem] += 16
        self.eng[q].dma_start(out=out, in_=in_).then_inc(self.semh[sem], 16)
        self._record((sem, self.cnt[sem], "dma"), "dma:" + sem, reads, writes)

    def wait_all(self, e, skip=()):
        for sk, v in self.cnt.items():
            if v > 0 and sk not in skip and self.seen[e].get(sk, 0) < v:
                self.eng[e].wait_ge(self.semh[sk], v)
                self.seen[e][sk] = v

    def clear_all(self, e):
        for sk in self.semh:
            self.eng[e].sem_clear(self.semh[sk])
            self.cnt[sk] = 0
        for k in self.seen:
            self.seen[k] = {}
        for b in self.bufs:
            b.w = None
            b.r = {}


def build_program(cfg, npar, poff, unit_rows, nrow_pad):
    KC, CA, CB, CF, T, D = cfg.KC, cfg.CA, cfg.CB, cfg.CF, cfg.T, cfg.D
    units = unit_plan(cfg)
    NU = len(units)
    nc = bass.Bass("TRN2", target_bir_lowering=False)
    xT = nc.dram_tensor("xT", [D, cfg.SP], F32, kind="ExternalInput").ap()
    wall = nc.dram_tensor("wall", [nrow_pad, 128], F32, kind="ExternalInput").ap()
    pvd = nc.dram_tensor("pv", [128, npar], F32, kind="ExternalInput").ap()
    posd = nc.dram_tensor("pos", [128, cfg.SP], F32, kind="ExternalInput").ap()
    H = cfg.H
    outT = nc.dram_tensor("outT", [D, cfg.SP], F32, kind="ExternalOutput").ap()
    PR = 524288
    npage = -(-nrow_pad // PR)
    wbfs = [nc.dram_tensor(f"wbf{k}", [min(PR, nrow_pad - k * PR), 128], BF16).ap() for k in range(npage)]

    def wrows(r0, n):
        k = r0 // PR
        assert (r0 + n - 1) // PR == k, "weight block straddles a DRAM page"
        return wbfs[k][r0 - k * PR:r0 - k * PR + n, :]
    xv = xT.rearrange("(c p) t -> p c t", p=128)
    ov = outT.rearrange("(c p) t -> p c t", p=128)
    TW = T + HIST

    with contextlib.ExitStack() as st:
        sb = lambda name, shape, dt: st.enter_context(nc.sbuf_tensor(name, shape, dt))
        x = sb("x", [128, KC, T], F32)
        h = sb("h", [128, KC, T], BF16)
        wsl = sb("wsl", [128, NSLOT, KC * 128], BF16)
        NYA = max(CA, 2 * GSZ)
        ya = sb("ya", [128, NYA, T], BF16)
        uc = sb("uc", [128, max(CB, 1), T], F32)
        tmp = sb("tmp", [128, NTMP, TW], F32)
        sqb = sb("sqb", [128, 4, T], BF16)
        stat = sb("stat", [128, 3, T], F32)
        ones = sb("ones", [128, 128], BF16)
        pv = sb("pvs", [128, npar], F32)
        st_f = sb("st_f", [128, cfg.DEPTH * 2 * CF, 2], F32)
        st_a = sb("st_a", [128, max(cfg.NE * CA, 1), 2], F32)
        st_b = sb("st_b", [128, max(cfg.NE * CB, 1), 30], F32)
        st_p = sb("st_p", [128, max(cfg.NO * KC, 1), 15], F32)
        rcnt = sb("rcnt", [128, 4, 16], F32)
        posw = sb("posw", [128, H + 16], F32)
        maskw = sb("maskw", [128, max(H, 1)], F32)
        ps = st.enter_context(nc.psum_tensor("ps", [128, 8, 512], F32))
        barB = st.enter_context(nc.semaphore("barB"))
        NCG = 16
        cvs = [st.enter_context(nc.semaphore(f"cvs{g}")) for g in range(NCG)]

        sy = Sync(nc, st)
        xb = [sy.buf() for _ in range(KC)]
        hb = [sy.buf() for _ in range(KC)]
        wb = [sy.buf() for _ in range(NSLOT)]
        wsem = [sy.dma_sem(f"w{s}") for s in range(NSLOT)]
        yab = [sy.buf() for _ in range(NYA)]
        ucb = [sy.buf() for _ in range(max(CB, 1))]
        tb = [sy.buf() for _ in range(NTMP)]
        sqbb = [sy.buf() for _ in range(4)]
        statb = [sy.buf() for _ in range(3)]
        psb = [sy.buf() for _ in range(8)]
        stb = sy.buf()
        pvb = sy.buf()
        onesb = sy.buf()
        xsem = sy.dma_sem("xld")
        osem = sy.dma_sem("ost")
        psem = sy.dma_sem("pld")
        possem = sy.dma_sem("posld")
        posb = sy.buf()
        maskb = sy.buf()

        V, A, PE = nc.vector, nc.scalar, nc.tensor

        def P(name, c=0, n=1):
            o = poff[name] + c
            return pv[:, o:o + n]

        CVB = 128 * 64
        ncv = nrow_pad // CVB
        per_g = -(-ncv // NCG)
        cvn = [0] * NCG
        for k in range(ncv):
            src = wall[k * CVB:(k + 1) * CVB, :].rearrange("(p j) f -> p (j f)", p=128)
            dst = wrows(k * CVB, CVB).rearrange("(p j) f -> p (j f)", p=128)
            nc.gpsimd.dma_start(out=dst, in_=src).then_inc(cvs[k // per_g], 16)
            cvn[k // per_g] += 16
        cv_waited = set()
        sy.dma("sp", psem, pv[:, :], pvd, writes=[pvb])
        sy.op("dve", lambda: V.memset(ones[:, :], 1.0), writes=[onesb])
        sy.op("dve", lambda: V.memset(st_f[:, :, :], 0.0), writes=[stb])
        sy.op("dve", lambda: V.memset(st_a[:, :, :], 0.0), writes=[stb])
        sy.op("dve", lambda: V.memset(st_b[:, :, :], 0.0), writes=[stb])
        sy.op("dve", lambda: V.memset(st_p[:, :, :], 0.0), writes=[stb])
        sy.wait_all("sp")
        sy.clear_all("sp")
        nc.sync.sem_inc(barB, 1)
        for e in ("pe", "act", "dve"):
            sy.eng[e].wait_ge(barB, 1)

        state = {"bank": 0, "tmp": 0, "sq": 0, "next_load": 0, "next_use": 0}

        def bank():
            b = state["bank"]
            state["bank"] = (b + 1) % 6
            return ps[:, b, 0:T], psb[b]

        def tmpb():
            k = state["tmp"]
            state["tmp"] = (k + 1) % NTMP
            return tmp[:, k, :], tb[k]

        def sqt():
            k = state["sq"]
            state["sq"] = (k + 1) % 4
            return sqb[:, k, :], sqbb[k]

        def load_unit(u):
            kind, li, a, b, nk = units[u]
            s = u % NSLOT
            r0 = unit_rows[u]
            src = wrows(r0, 128 * nk).rearrange("(p k) f -> p (k f)", p=128)
            for g in range((r0 // CVB) // per_g, ((r0 + 128 * nk - 1) // CVB) // per_g + 1):
                if g not in cv_waited:
                    cv_waited.add(g)
                    nc.sync.wait_ge(cvs[g], cvn[g])
            sy.dma("sp", wsem[s], wsl[:, s, 0:nk * 128], src, writes=[wb[s]])

        def prefetch(upto):
            while state["next_load"] < min(upto, NU):
                load_unit(state["next_load"])
                state["next_load"] += 1

        def take_unit(kind):
            u = state["next_use"]
            assert units[u][0] == kind, (units[u], kind)
            state["next_use"] += 1
            s = u % NSLOT
            return u, wsl[:, s, :].rearrange("p (k f) -> p k f", f=128), wb[s]

        def release(u_last):
            prefetch(u_last + 1 + NSLOT)

        def mm_group(bk, bkb, pairs, reads):
            def emit():
                ins = None
                n = len(pairs)
                for k, (l_, r_) in enumerate(pairs):
                    ins = PE.matmul(bk, l_, r_, start=(k == 0), stop=(k == n - 1))
                return ins
            sy.op("pe", emit, reads=reads, writes=[bkb])

        def proj(kind, rhs):
            u, wv, wbuf = take_unit(kind)
            bk, bkb = bank()
            mm_group(bk, bkb, [(wv[:, k, :], r[0]) for k, r in enumerate(rhs)],
                     [wbuf] + [r[1] for r in rhs])
            release(u)
            return bk, bkb

        def colsum(terms, scale_bias=None):
            bk, bkb = bank()
            n = len(terms)
            for k, mk in enumerate(terms):
                ap, b = mk()
                sy.op("pe", lambda ap=ap, k=k: PE.matmul(bk, ones[:, :], ap, start=(k == 0), stop=(k == n - 1)),
                      reads=[b, onesb], writes=[bkb])
            return bk, bkb

        def rstd_from(bk, bkb, n):
            t, tb_ = stat[:, 0, :], statb[0]
            sy.op("dve", lambda: V.tensor_scalar(out=t[:, 0:T], in0=bk, scalar1=1.0 / n, scalar2=EPS,
                                                 op0=ALU.mult, op1=ALU.add), reads=[bkb], writes=[tb_])
            sy.op("act", lambda: A.activation(out=t[:, 0:T], in_=t[:, 0:T], func=AF.Sqrt), reads=[tb_], writes=[tb_])
            sy.op("dve", lambda: V.reciprocal(out=t[:, 0:T], in_=t[:, 0:T]), reads=[tb_], writes=[tb_])
            return t, tb_

        def rms_rstd():
            def mk(c):
                def f():
                    s_, sb_ = sqt()
                    sy.op("act", lambda: A.activation(out=s_, in_=x[:, c, :], func=AF.Square),
                          reads=[xb[c]], writes=[sb_])
                    return s_, sb_
                return f
            bk, bkb = colsum([mk(c) for c in range(KC)])
            return rstd_from(bk, bkb, D)

        def norm_to_h(gname):
            r, rb = rms_rstd()
            for c in range(KC):
                sy.op("dve", lambda c=c: V.scalar_tensor_tensor(
                    out=h[:, c, :], in0=x[:, c, :], scalar=P(gname, c), in1=r[:, 0:T],
                    op0=ALU.mult, op1=ALU.mult), reads=[xb[c], rb, pvb], writes=[hb[c]])

        def conv_taps(acc, accb, ub, ubb, K, wname, wstride, widx, bias=None):
            wcol = lambda k: P(wname, k * wstride + widx)
            if bias is None:
                sy.op("dve", lambda: V.tensor_scalar(out=acc[:, 0:T], in0=ub[:, K - 1:K - 1 + T], scalar1=wcol(K - 1),
                                                     scalar2=None, op0=ALU.mult), reads=[ubb, pvb], writes=[accb])
            else:
                sy.op("dve", lambda: V.tensor_scalar(out=acc[:, 0:T], in0=ub[:, K - 1:K - 1 + T], scalar1=wcol(K - 1),
                                                     scalar2=bias, op0=ALU.mult, op1=ALU.add),
                      reads=[ubb, pvb], writes=[accb])
            for k in range(K - 2, -1, -1):
                sy.op("dve", lambda k=k: V.scalar_tensor_tensor(
                    out=acc[:, 0:T], in0=ub[:, k:k + T], scalar=wcol(k), in1=acc[:, 0:T],
                    op0=ALU.mult, op1=ALU.add), reads=[ubb, accb, pvb], writes=[accb])

        def history(ub, ubb, stash_ap, H):
            sy.op("dve", lambda: V.tensor_copy(out=ub[:, 0:H], in_=stash_ap), reads=[stb], writes=[ubb])
            sy.op("dve", lambda: V.tensor_copy(out=stash_ap, in_=ub[:, T:T + H]), reads=[ubb], writes=[stb])

        def mask_x():
            if H == 0:
                return
            for c in range(KC):
                sy.op("dve", lambda c=c: V.tensor_tensor(out=x[:, c, 0:H], in0=x[:, c, 0:H], in1=maskw[:, 0:H], op=ALU.mult),
                      reads=[xb[c], maskb], writes=[xb[c]])

        def ffn(l):
            norm_to_h(f"ffnn{l}")
            hr = [(h[:, c, :], hb[c]) for c in range(KC)]
            gi = 0
            for j0 in range(0, CF, GSZ):
                grp = list(range(j0, min(j0 + GSZ, CF)))
                gaps = []
                for idx, j in enumerate(grp):
                    accs = []
                    for part, kind in ((0, "fg"), (1, "fu")):
                        col = part * CF + j
                        bk, bkb = proj(kind, hr)
                        ub, ubb = tmpb()
                        sy.op("act", lambda: A.activation(out=ub[:, 2:2 + T], in_=bk, func=AF.Copy),
                              reads=[bkb], writes=[ubb])
                        history(ub, ubb, st_f[:, l * 2 * CF + col, :], 2)
                        acc, accb = tmpb()
                        conv_taps(acc, accb, ub, ubb, 3, f"fcw{l}", 2 * CF, col, bias=P(f"fcb{l}", col))
                        accs.append((acc, accb))
                    (ag, agb), (au, aub) = accs
                    sy.op("act", lambda: A.activation(out=ag[:, 0:T], in_=ag[:, 0:T], func=AF.Silu),
                          reads=[agb], writes=[agb])
                    gs = (gi % 2) * GSZ + idx
                    sy.op("dve", lambda: V.tensor_tensor(out=ya[:, gs, :], in0=ag[:, 0:T], in1=au[:, 0:T], op=ALU.mult),
                          reads=[agb, aub], writes=[yab[gs]])
                    gaps.append((ya[:, gs, :], yab[gs]))
                dus = [take_unit("fd") for _ in grp]
                for d in range(KC):
                    bk, bkb = bank()
                    mm_group(bk, bkb, [(dus[i][1][:, d, :], gaps[i][0]) for i in range(len(grp))],
                             [du[2] for du in dus] + [g_[1] for g_ in gaps])
                    sy.op("dve", lambda d=d, bk=bk: V.tensor_tensor(out=x[:, d, :], in0=x[:, d, :], in1=bk, op=ALU.add),
                          reads=[xb[d], bkb], writes=[xb[d]])
                release(dus[-1][0])
                gi += 1

        def hybrid(l):
            i = l // 2
            norm_to_h(f"mixn{l}")
            hr = [(h[:, c, :], hb[c]) for c in range(KC)]
            s1, s1b = ps[:, 6, 0:T], psb[6]
            s2, s2b = ps[:, 7, 0:T], psb[7]
            def part_b(c):
                bv, bvb = proj("bv", hr)
                bg, bgb = proj("bg", hr)
                t1, t1b = tmpb()
                t2, t2b = tmpb()
                sy.op("act", lambda: A.activation(out=t1[:, 0:T], in_=bg, func=AF.Tanh, scale=0.5),
                      reads=[bgb], writes=[t1b])
                sy.op("act", lambda: A.activation(out=t2[:, 0:T], in_=bv, func=AF.Identity, scale=0.5),
                      reads=[bvb], writes=[t2b])
                ub, ubb = tmpb()
                sy.op("dve", lambda: V.scalar_tensor_tensor(out=ub[:, 30:30 + T], in0=t1[:, 0:T], scalar=1.0,
                                                            in1=t2[:, 0:T], op0=ALU.add, op1=ALU.mult),
                      reads=[t1b, t2b], writes=[ubb])
                history(ub, ubb, st_b[:, i * CB + c, :], 30)
                conv_taps(uc[:, c, :], ucb[c], ub, ubb, B_CONV_W, f"cbw{i}", CB, c, bias=P(f"cbb{i}", c))
                q1, q1b = sqt()
                q2, q2b = sqt()
                sy.op("act", lambda: A.activation(out=q1, in_=uc[:, c, :], func=AF.Copy), reads=[ucb[c]], writes=[q1b])
                sy.op("act", lambda: A.activation(out=q2, in_=uc[:, c, :], func=AF.Square), reads=[ucb[c]], writes=[q2b])
                sy.op("pe", lambda: PE.matmul(s1, ones[:, :], q1, start=(c == 0), stop=(c == CB - 1)),
                      reads=[q1b, onesb], writes=[s1b])
                sy.op("pe", lambda: PE.matmul(s2, ones[:, :], q2, start=(c == 0), stop=(c == CB - 1)),
                      reads=[q2b, onesb], writes=[s2b])
            def part_a(c):
                ax, axb = proj("ax", hr)
                ac, acb = proj("ac", hr)
                ab, abb = proj("ab", hr)
                t1, t1b = tmpb()
                sy.op("act", lambda: A.activation(out=t1[:, 0:T], in_=ax, func=AF.Copy), reads=[axb], writes=[t1b])
                ub, ubb = tmpb()
                sy.op("dve", lambda: V.tensor_tensor(out=ub[:, 2:2 + T], in0=ac, in1=t1[:, 0:T], op=ALU.mult),
                      reads=[acb, t1b], writes=[ubb])
                history(ub, ubb, st_a[:, i * CA + c, :], 2)
                acc, accb = tmpb()
                conv_taps(acc, accb, ub, ubb, A_CONV_W, f"caw{i}", CA, c)
                sy.op("dve", lambda: V.tensor_tensor(out=ya[:, c, :], in0=acc[:, 0:T], in1=ab, op=ALU.mult),
                      reads=[accb, abb], writes=[yab[c]])
            for c in range(max(CA, CB)):
                if c < CB:
                    part_b(c)
                if c < CA:
                    part_a(c)
            mu, mub = stat[:, 1, :], statb[1]
            sy.op("dve", lambda: V.tensor_scalar(out=mu[:, 0:T], in0=s1, scalar1=1.0 / cfg.DB, scalar2=None, op0=ALU.mult),
                  reads=[s1b], writes=[mub])
            var, varb = stat[:, 2, :], statb[2]
            sy.op("dve", lambda: V.tensor_tensor(out=var[:, 0:T], in0=mu[:, 0:T], in1=mu[:, 0:T], op=ALU.mult),
                  reads=[mub], writes=[varb])
            sy.op("dve", lambda: V.scalar_tensor_tensor(out=var[:, 0:T], in0=s2, scalar=1.0 / cfg.DB, in1=var[:, 0:T],
                                                        op0=ALU.mult, op1=ALU.subtract),
                  reads=[s2b, varb], writes=[varb])
            sy.op("dve", lambda: V.tensor_scalar(out=var[:, 0:T], in0=var[:, 0:T], scalar1=EPS, scalar2=None, op0=ALU.add),
                  reads=[varb], writes=[varb])
            sy.op("act", lambda: A.activation(out=var[:, 0:T], in_=var[:, 0:T], func=AF.Sqrt), reads=[varb], writes=[varb])
            sy.op("dve", lambda: V.reciprocal(out=var[:, 0:T], in_=var[:, 0:T]), reads=[varb], writes=[varb])
            sy.op("dve", lambda: V.scalar_tensor_tensor(out=mu[:, 0:T], in0=mu[:, 0:T], scalar=-1.0, in1=var[:, 0:T],
                                                        op0=ALU.mult, op1=ALU.mult),
                  reads=[mub, varb], writes=[mub])
            for c in range(CB):
                z, zb = tmpb()
                sy.op("dve", lambda: V.tensor_tensor(out=z[:, 0:T], in0=uc[:, c, :], in1=var[:, 0:T], op=ALU.mult),
                      reads=[ucb[c], varb], writes=[zb])
                sy.op("dve", lambda: V.tensor_tensor(out=z[:, 0:T], in0=z[:, 0:T], in1=mu[:, 0:T], op=ALU.add),
                      reads=[zb, mub], writes=[zb])
                sy.op("act", lambda: A.activation(out=h[:, c, :], in_=z[:, 0:T], func=AF.Silu,
                                                  bias=P(f"lnb{i}", c), scale=P(f"lng{i}", c)),
                      reads=[zb, pvb], writes=[hb[c]])
            yr = [(ya[:, c, :], yab[c]) for c in range(CA)] + [(h[:, c, :], hb[c]) for c in range(CB)]
            for d in range(KC):
                bk, bkb = proj("wo", yr)
                sy.op("dve", lambda bk=bk: V.tensor_tensor(out=x[:, d, :], in0=x[:, d, :], in1=bk, op=ALU.add),
                      reads=[xb[d], bkb], writes=[xb[d]])

        def poolmix(l):
            i = l // 2
            PK = cfg.PK
            r, rb = rms_rstd()
            E = T + 15
            for c in range(KC):
                g = c // PK
                win = POOL_WINDOWS[g]
                hf, hfb = tmpb()
                sy.op("dve", lambda: V.scalar_tensor_tensor(out=hf[:, 15:E], in0=x[:, c, :], scalar=P(f"mixn{l}", c),
                                                            in1=r[:, 0:T], op0=ALU.mult, op1=ALU.mult),
                      reads=[xb[c], rb, pvb], writes=[hfb])
                history(hf, hfb, st_p[:, i * KC + c, :], 15)
                s_, s_b = hf, hfb
                lo, off = 0, 1
                while off < win:
                    n_, n_b = tmpb()
                    lo2 = lo + off
                    sy.op("dve", lambda s_=s_, n_=n_, lo2=lo2, off=off: V.tensor_tensor(
                        out=n_[:, lo2:E], in0=s_[:, lo2:E], in1=s_[:, lo2 - off:E - off], op=ALU.add),
                        reads=[s_b], writes=[n_b])
                    s_, s_b, lo, off = n_, n_b, lo2, off * 2
                sy.op("dve", lambda: V.scalar_tensor_tensor(out=h[:, c, :], in0=s_[:, 15:E], scalar=1.0 / win,
                                                            in1=hf[:, 15:E], op0=ALU.mult, op1=ALU.subtract),
                      reads=[s_b, hfb], writes=[hb[c]])
                t16, t16b = tmpb()
                sy.op("dve", lambda: V.tensor_tensor(out=t16[:, 0:16], in0=s_[:, 15 + H:31 + H], in1=rcnt[:, g, :], op=ALU.mult),
                      reads=[s_b, stb], writes=[t16b])
                sy.op("dve", lambda: V.tensor_tensor(out=h[:, c, H:H + 16], in0=t16[:, 0:16], in1=hf[:, 15 + H:31 + H], op=ALU.subtract),
                      reads=[t16b, hfb, hb[c]], writes=[hb[c]])
            for g in range(4):
                rhs = [(h[:, g * PK + k, :], hb[g * PK + k]) for k in range(PK)]
                for dl in range(PK):
                    d = g * PK + dl
                    bk, bkb = proj("pl", rhs)
                    sy.op("dve", lambda bk=bk: V.scalar_tensor_tensor(out=x[:, d, :], in0=bk, scalar=P(f"psc{i}", d),
                                                                      in1=x[:, d, :], op0=ALU.mult, op1=ALU.add),
                          reads=[bkb, xb[d], pvb], writes=[xb[d]])

        engs = [ET.SP, ET.Activation, ET.DVE, ET.PE]
        with nc.Fori(0, cfg.NT, engines=engs) as it:
            sy.dma("sp", xsem, x[:, :, :], xv[:, :, bass.ds(it * T, T)], writes=xb)
            sy.dma("sp", possem, posw[:, :], posd[:, bass.ds(it * T, H + 16)], writes=[posb])
            prefetch(NSLOT)
            if H > 0:
                sy.op("dve", lambda: V.tensor_scalar(out=maskw[:, 0:H], in0=posw[:, 0:H], scalar1=0.0, scalar2=None,
                                                     op0=ALU.is_ge), reads=[posb], writes=[maskb])
            for g, win in enumerate(POOL_WINDOWS):
                sy.op("dve", lambda g=g: V.tensor_scalar(out=rcnt[:, g, :], in0=posw[:, H:H + 16], scalar1=1.0, scalar2=1.0,
                                                         op0=ALU.add, op1=ALU.max), reads=[posb], writes=[stb])
                sy.op("dve", lambda g=g, win=win: V.tensor_scalar(out=rcnt[:, g, :], in0=rcnt[:, g, :], scalar1=float(win),
                                                                  scalar2=None, op0=ALU.min), reads=[stb], writes=[stb])
                sy.op("dve", lambda g=g: V.reciprocal(out=rcnt[:, g, :], in_=rcnt[:, g, :]), reads=[stb], writes=[stb])
            for l in range(cfg.DEPTH):
                if l % 2 == 0:
                    hybrid(l)
                else:
                    poolmix(l)
                mask_x()
                ffn(l)
                mask_x()
            assert state["next_use"] == NU and state["next_load"] == NU
            r, rb = rms_rstd()
            for c in range(KC):
                sy.op("dve", lambda c=c: V.scalar_tensor_tensor(
                    out=x[:, c, :], in0=x[:, c, :], scalar=P("finn", c), in1=r[:, 0:T],
                    op0=ALU.mult, op1=ALU.mult), reads=[xb[c], rb, pvb], writes=[xb[c]])
            sy.dma("sp", osem, ov[:, :, bass.ds(it * T, T)], x[:, :, :], reads=xb)
            sy.wait_all("sp")
            sy.clear_all("sp")
            nc.sync.sem_inc(barB, 1)
            for e in ("pe", "act", "dve"):
                sy.eng[e].wait_ge(barB, it + 2)
    return nc


FULL = dict(D=4096, DA=2048, DB=2048, DFF=11008, DEPTH=4, S=16384, T=360, NCORES=8)


def run(cfg, inp, trace=False):
    n, H, SC = cfg.NCORES, cfg.H, cfg.SC
    xs = np.asarray(inp["x"], dtype=np.float32).reshape(cfg.S, cfg.D)
    width = (n - 1) * SC + cfg.SP
    xfull = np.zeros((cfg.D, width), np.float32)
    xfull[:, H:H + cfg.S] = xs.T
    wall, rows = pack_weights(cfg, inp)
    pk = pack_params(cfg, inp)
    pvn = pk.build()
    nc = build_program(cfg, pvn.shape[1], pk.off, rows, wall.shape[0])
    in_maps = []
    for c in range(n):
        pos = (c * SC - H + np.arange(cfg.SP)).astype(np.float32)
        in_maps.append({"xT": np.ascontiguousarray(xfull[:, c * SC:c * SC + cfg.SP]), "wall": wall, "pv": pvn,
                        "pos": np.ascontiguousarray(np.broadcast_to(pos[None, :], (128, cfg.SP)))})
    res = run_bass_kernel_spmd(nc, in_maps, core_ids=list(range(n)), trace=trace)
    out = np.empty((cfg.S, cfg.D), np.float32)
    for c in range(n):
        oT = np.asarray(res.results[c]["outT"])
        out[c * SC:(c + 1) * SC, :] = oT[:, H:H + SC].T
    return out.reshape(1, cfg.S, cfg.D), res


def kernel(**inputs):
    cfg = Cfg(**FULL)
    out, _ = run(cfg, inputs)
    return out
```

```python
import contextlib
import numpy as np
import concourse.bass as bass
import concourse.mybir as mybir
from concourse.bass_utils import run_bass_kernel_spmd

F32 = mybir.dt.float32
BF16 = mybir.dt.bfloat16
AF = mybir.ActivationFunctionType
ALU = mybir.AluOpType
ET = mybir.EngineType

EPS = 1e-6
POOL_WINDOWS = (2, 4, 8, 16)
A_CONV_W = 3
B_CONV_W = 31
FFN_CONV_W = 3
GSZ = 4
NSLOT = 6
NTMP = 7
HIST = 32


class Cfg:
    def __init__(self, D, DA, DB, DFF, DEPTH, S, T, NCORES=1):
        self.D, self.DA, self.DB, self.DFF, self.DEPTH, self.S, self.T = D, DA, DB, DFF, DEPTH, S, T
        self.NCORES = NCORES
        self.KC, self.CA, self.CB, self.CF = D // 128, DA // 128, DB // 128, DFF // 128
        assert D % 512 == 0 and DA % 128 == 0 and DB % 128 == 0 and DFF % 128 == 0 and DA + DB == D
        assert S % NCORES == 0
        self.SC = S // NCORES
        self.H = 0 if NCORES == 1 else sum((B_CONV_W - 1 if l % 2 == 0 else POOL_WINDOWS[-1] - 1)
                                           + FFN_CONV_W - 1 for l in range(DEPTH))
        assert self.H + 16 <= T
        self.NT = -(-(self.H + self.SC) // T)
        self.SP = self.NT * T
        self.NE = (DEPTH + 1) // 2
        self.NO = DEPTH // 2
        self.PK = self.KC // 4


def unit_plan(cfg):
    units = []
    for l in range(cfg.DEPTH):
        i = l // 2
        if l % 2 == 0:
            for c in range(max(cfg.CA, cfg.CB)):
                if c < cfg.CB:
                    units.append(("bv", i, c, 0, cfg.KC))
                    units.append(("bg", i, c, 0, cfg.KC))
                if c < cfg.CA:
                    units.append(("ax", i, c, 0, cfg.KC))
                    units.append(("ac", i, c, 0, cfg.KC))
                    units.append(("ab", i, c, 0, cfg.KC))
            for d in range(cfg.KC):
                units.append(("wo", i, d, 0, cfg.KC))
        else:
            for g in range(4):
                for dl in range(cfg.PK):
                    units.append(("pl", i, g, dl, cfg.PK))
        for j0 in range(0, cfg.CF, GSZ):
            grp = list(range(j0, min(j0 + GSZ, cfg.CF)))
            for j in grp:
                units.append(("fg", l, j, 0, cfg.KC))
                units.append(("fu", l, j, 0, cfg.KC))
            for j in grp:
                units.append(("fd", l, j, 0, cfg.KC))
    return units


def _colunit(W, f0):
    k = W.shape[0] // 128
    return W[:, f0:f0 + 128].reshape(k, 128, 128).transpose(1, 0, 2)


def pack_weights(cfg, inp):
    units = unit_plan(cfg)
    rows = [0]
    for u in units:
        rows.append(rows[-1] + 128 * u[4])
    nrow = rows[-1]
    CVB = 128 * 64
    nrow_pad = -(-nrow // CVB) * CVB
    wall = np.zeros((nrow_pad, 128), np.float32)
    DA, DB, DFF = cfg.DA, cfg.DB, cfg.DFF
    for u, r0 in zip(units, rows[:-1]):
        kind, li, a, b, nk = u
        if kind in ("ax", "ac", "ab", "bv", "bg"):
            base = {"ax": 0, "ac": DA, "ab": 2 * DA, "bv": 3 * DA, "bg": 3 * DA + DB}[kind]
            blk = _colunit(inp["hyb_w_in"][li], base + a * 128)
        elif kind == "wo":
            blk = _colunit(inp["hyb_w_out"][li], a * 128)
        elif kind == "pl":
            blk = _colunit(inp["pool_w"][li][a], b * 128)
        elif kind == "fg":
            blk = _colunit(inp["ffn_w_up"][li], a * 128)
        elif kind == "fu":
            blk = _colunit(inp["ffn_w_up"][li], DFF + a * 128)
        elif kind == "fd":
            blk = inp["ffn_w_down"][li][a * 128:(a + 1) * 128, :].reshape(128, cfg.KC, 128)
        wall[r0:r0 + 128 * nk, :] = blk.reshape(128 * nk, 128)
    return wall, rows


class ParamPack:
    def __init__(self):
        self.cols = []
        self.off = {}
        self.n = 0

    def add(self, name, arr):
        arr = np.ascontiguousarray(arr, dtype=np.float32).reshape(128, -1)
        self.off[name] = self.n
        self.n += arr.shape[1]
        self.cols.append(arr)

    def build(self):
        return np.concatenate(self.cols, axis=1)


def _vec(v):
    return np.asarray(v).reshape(-1, 128).T


def _taps(w):
    K, C = w.shape
    return np.asarray(w).reshape(K, C // 128, 128).transpose(2, 0, 1).reshape(128, -1)


def pack_params(cfg, inp, with_data=True):
    pk = ParamPack()
    z = lambda *s: np.zeros(s, np.float32)
    for l in range(cfg.DEPTH):
        pk.add(f"mixn{l}", _vec(inp["mix_norm"][l]) if with_data else z(128, cfg.KC))
        pk.add(f"ffnn{l}", _vec(inp["ffn_norm"][l]) if with_data else z(128, cfg.KC))
        pk.add(f"fcw{l}", _taps(inp["ffn_conv"][l]) if with_data else z(128, 3 * 2 * cfg.CF))
        pk.add(f"fcb{l}", _vec(inp["ffn_conv_bias"][l]) if with_data else z(128, 2 * cfg.CF))
    pk.add("finn", _vec(inp["final_norm"]) if with_data else z(128, cfg.KC))
    for i in range(cfg.NE):
        pk.add(f"caw{i}", _taps(inp["hyb_conv_a"][i]) if with_data else z(128, A_CONV_W * cfg.CA))
        pk.add(f"cbw{i}", _taps(inp["hyb_conv_b"][i]) if with_data else z(128, B_CONV_W * cfg.CB))
        pk.add(f"cbb{i}", _vec(inp["hyb_conv_b_bias"][i]) if with_data else z(128, cfg.CB))
        pk.add(f"lng{i}", _vec(inp["hyb_ln_g"][i]) if with_data else z(128, cfg.CB))
        pk.add(f"lnb{i}", _vec(inp["hyb_ln_b"][i]) if with_data else z(128, cfg.CB))
    for i in range(cfg.NO):
        pk.add(f"psc{i}", _vec(inp["pool_scale"][i]) if with_data else z(128, cfg.KC))
    pk.add("pos16", np.tile(np.arange(1, 17, dtype=np.float32)[None, :], (128, 1)))
    return pk


class Buf:
    __slots__ = ("w", "r")

    def __init__(self):
        self.w = None
        self.r = {}


class Sync:
    def __init__(self, nc, stack):
        self.nc = nc
        self.eng = {"pe": nc.tensor, "act": nc.scalar, "dve": nc.vector, "sp": nc.sync, "pool": nc.gpsimd}
        self.semh = {}
        self.cnt = {}
        for k in self.eng:
            self.semh[k] = stack.enter_context(nc.semaphore("i_" + k))
            self.cnt[k] = 0
        self.stack = stack
        self.seen = {k: {} for k in self.eng}
        self.bufs = []

    def buf(self):
        b = Buf()
        self.bufs.append(b)
        return b

    def dma_sem(self, name):
        self.semh[name] = self.stack.enter_context(self.nc.semaphore("d_" + name))
        self.cnt[name] = 0
        return name

    def _need(self, e, reads, writes):
        ev = []
        for b in reads:
            if b.w is not None:
                ev.append(b.w)
        for b in writes:
            if b.w is not None and b.w[2] != e:
                ev.append(b.w)
            for r in b.r.values():
                if r[2] != e:
                    ev.append(r)
        return ev

    def _wait(self, e, events):
        best = {}
        for (sk, v, _pe) in events:
            if v > best.get(sk, 0):
                best[sk] = v
        for sk, v in best.items():
            if self.seen[e].get(sk, 0) >= v:
                continue
            self.eng[e].wait_ge(self.semh[sk], v)
            self.seen[e][sk] = v

    def _record(self, ev, ekey, reads, writes):
        for b in reads:
            b.r[ekey] = ev
        for b in writes:
            b.w = ev
            b.r = {}

    def op(self, e, emit, reads=(), writes=()):
        self._wait(e, self._need(e, reads, writes))
        ins = emit()
        self.cnt[e] += 1
        ins.then_inc(self.semh[e], 1)
        self._record((e, self.cnt[e], e), e, reads, writes)

    def dma(self, q, sem, out, in_, reads=(), writes=()):
        self._wait(q, self._need(None, reads, writes))
        self.cnt[sem] += 16
        self.eng[q].dma_start(out=out, in_=in_).then_inc(self.semh[sem], 16)
        self._record((sem, self.cnt[sem], "dma"), "dma:" + sem, reads, writes)

    def wait_all(self, e, skip=()):
        for sk, v in self.cnt.items():
            if v > 0 and sk not in skip and self.seen[e].get(sk, 0) < v:
                self.eng[e].wait_ge(self.semh[sk], v)
                self.seen[e][sk] = v

    def clear_all(self, e):
        for sk in self.semh:
            self.eng[e].sem_clear(self.semh[sk])
            self.cnt[sk] = 0
        for k in self.seen:
            self.seen[k] = {}
        for b in self.bufs:
            b.w = None
            b.r = {}


def build_program(cfg, npar, poff, unit_rows, nrow_pad):
    KC, CA, CB, CF, T, D = cfg.KC, cfg.CA, cfg.CB, cfg.CF, cfg.T, cfg.D
    units = unit_plan(cfg)
    NU = len(units)
    nc = bass.Bass("TRN2", target_bir_lowering=False)
    xT = nc.dram_tensor("xT", [D, cfg.SP], F32, kind="ExternalInput").ap()
    wall = nc.dram_tensor("wall", [nrow_pad, 128], F32, kind="ExternalInput").ap()
    pvd = nc.dram_tensor("pv", [128, npar], F32, kind="ExternalInput").ap()
    posd = nc.dram_tensor("pos", [128, cfg.SP], F32, kind="ExternalInput").ap()
    H = cfg.H
    outT = nc.dram_tensor("outT", [D, cfg.SP], F32, kind="ExternalOutput").ap()
    PR = 524288
    npage = -(-nrow_pad // PR)
    wbfs = [nc.dram_tensor(f"wbf{k}", [min(PR, nrow_pad - k * PR), 128], BF16).ap() for k in range(npage)]

    def wrows(r0, n):
        k = r0 // PR
        assert (r0 + n - 1) // PR == k, "weight block straddles a DRAM page"
        return wbfs[k][r0 - k * PR:r0 - k * PR + n, :]
    xv = xT.rearrange("(c p) t -> p c t", p=128)
    ov = outT.rearrange("(c p) t -> p c t", p=128)
    TW = T + HIST

    with contextlib.ExitStack() as st:
        sb = lambda name, shape, dt: st.enter_context(nc.sbuf_tensor(name, shape, dt))
        x = sb("x", [128, KC, T], F32)
        h = sb("h", [128, KC, T], BF16)
        wsl = sb("wsl", [128, NSLOT, KC * 128], BF16)
        NYA = max(CA, 2 * GSZ)
        ya = sb("ya", [128, NYA, T], BF16)
        uc = sb("uc", [128, max(CB, 1), T], F32)
        tmp = sb("tmp", [128, NTMP, TW], F32)
        sqb = sb("sqb", [128, 4, T], BF16)
        stat = sb("stat", [128, 3, T], F32)
        ones = sb("ones", [128, 128], BF16)
        pv = sb("pvs", [128, npar], F32)
        st_f = sb("st_f", [128, cfg.DEPTH * 2 * CF, 2], F32)
        st_a = sb("st_a", [128, max(cfg.NE * CA, 1), 2], F32)
        st_b = sb("st_b", [128, max(cfg.NE * CB, 1), 30], F32)
        st_p = sb("st_p", [128, max(cfg.NO * KC, 1), 15], F32)
        rcnt = sb("rcnt", [128, 4, 16], F32)
        posw = sb("posw", [128, H + 16], F32)
        maskw = sb("maskw", [128, max(H, 1)], F32)
        ps = st.enter_context(nc.psum_tensor("ps", [128, 8, 512], F32))
        barB = st.enter_context(nc.semaphore("barB"))
        NCG = 16
        cvs = [st.enter_context(nc.semaphore(f"cvs{g}")) for g in range(NCG)]

        sy = Sync(nc, st)
        xb = [sy.buf() for _ in range(KC)]
        hb = [sy.buf() for _ in range(KC)]
        wb = [sy.buf() for _ in range(NSLOT)]
        wsem = [sy.dma_sem(f"w{s}") for s in range(NSLOT)]
        yab = [sy.buf() for _ in range(NYA)]
        ucb = [sy.buf() for _ in range(max(CB, 1))]
        tb = [sy.buf() for _ in range(NTMP)]
        sqbb = [sy.buf() for _ in range(4)]
        statb = [sy.buf() for _ in range(3)]
        psb = [sy.buf() for _ in range(8)]
        stb = sy.buf()
        pvb = sy.buf()
        onesb = sy.buf()
        xsem = sy.dma_sem("xld")
        osem = sy.dma_sem("ost")
        psem = sy.dma_sem("pld")
        possem = sy.dma_sem("posld")
        posb = sy.buf()
        maskb = sy.buf()

        V, A, PE = nc.vector, nc.scalar, nc.tensor

        def P(name, c=0, n=1):
            o = poff[name] + c
            return pv[:, o:o + n]

        CVB = 128 * 64
        ncv = nrow_pad // CVB
        per_g = -(-ncv // NCG)
        cvn = [0] * NCG
        for k in range(ncv):
            src = wall[k * CVB:(k + 1) * CVB, :].rearrange("(p j) f -> p (j f)", p=128)
            dst = wrows(k * CVB, CVB).rearrange("(p j) f -> p (j f)", p=128)
            nc.gpsimd.dma_start(out=dst, in_=src).then_inc(cvs[k // per_g], 16)
            cvn[k // per_g] += 16
        cv_waited = set()
        sy.dma("sp", psem, pv[:, :], pvd, writes=[pvb])
        sy.op("dve", lambda: V.memset(ones[:, :], 1.0), writes=[onesb])
        sy.op("dve", lambda: V.memset(st_f[:, :, :], 0.0), writes=[stb])
        sy.op("dve", lambda: V.memset(st_a[:, :, :], 0.0), writes=[stb])
        sy.op("dve", lambda: V.memset(st_b[:, :, :], 0.0), writes=[stb])
        sy.op("dve", lambda: V.memset(st_p[:, :, :], 0.0), writes=[stb])
        sy.wait_all("sp")
        sy.clear_all("sp")
        nc.sync.sem_inc(barB, 1)
        for e in ("pe", "act", "dve"):
            sy.eng[e].wait_ge(barB, 1)

        state = {"bank": 0, "tmp": 0, "sq": 0, "next_load": 0, "next_use": 0}

        def bank():
            b = state["bank"]
            state["bank"] = (b + 1) % 6
            return ps[:, b, 0:T], psb[b]

        def tmpb():
            k = state["tmp"]
            state["tmp"] = (k + 1) % NTMP
            return tmp[:, k, :], tb[k]

        def sqt():
            k = state["sq"]
            state["sq"] = (k + 1) % 4
            return sqb[:, k, :], sqbb[k]

        def load_unit(u):
            kind, li, a, b, nk = units[u]
            s = u % NSLOT
            r0 = unit_rows[u]
            src = wrows(r0, 128 * nk).rearrange("(p k) f -> p (k f)", p=128)
            for g in range((r0 // CVB) // per_g, ((r0 + 128 * nk - 1) // CVB) // per_g + 1):
                if g not in cv_waited:
                    cv_waited.add(g)
                    nc.sync.wait_ge(cvs[g], cvn[g])
            sy.dma("sp", wsem[s], wsl[:, s, 0:nk * 128], src, writes=[wb[s]])

        def prefetch(upto):
            while state["next_load"] < min(upto, NU):
                load_unit(state["next_load"])
                state["next_load"] += 1

        def take_unit(kind):
            u = state["next_use"]
            assert units[u][0] == kind, (units[u], kind)
            state["next_use"] += 1
            s = u % NSLOT
            return u, wsl[:, s, :].rearrange("p (k f) -> p k f", f=128), wb[s]

        def release(u_last):
            prefetch(u_last + 1 + NSLOT)

        def mm_group(bk, bkb, pairs, reads):
            def emit():
                ins = None
                n = len(pairs)
                for k, (l_, r_) in enumerate(pairs):
                    ins = PE.matmul(bk, l_, r_, start=(k == 0), stop=(k == n - 1))
                return ins
            sy.op("pe", emit, reads=reads, writes=[bkb])

        def proj(kind, rhs):
            u, wv, wbuf = take_unit(kind)
            bk, bkb = bank()
            mm_group(bk, bkb, [(wv[:, k, :], r[0]) for k, r in enumerate(rhs)],
                     [wbuf] + [r[1] for r in rhs])
            release(u)
            return bk, bkb

        def colsum(terms, scale_bias=None):
            bk, bkb = bank()
            n = len(terms)
            for k, mk in enumerate(terms):
                ap, b = mk()
                sy.op("pe", lambda ap=ap, k=k: PE.matmul(bk, ones[:, :], ap, start=(k == 0), stop=(k == n - 1)),
                      reads=[b, onesb], writes=[bkb])
            return bk, bkb

        def rstd_from(bk, bkb, n):
            t, tb_ = stat[:, 0, :], statb[0]
            sy.op("dve", lambda: V.tensor_scalar(out=t[:, 0:T], in0=bk, scalar1=1.0 / n, scalar2=EPS,
                                                 op0=ALU.mult, op1=ALU.add), reads=[bkb], writes=[tb_])
            sy.op("act", lambda: A.activation(out=t[:, 0:T], in_=t[:, 0:T], func=AF.Sqrt), reads=[tb_], writes=[tb_])
            sy.op("dve", lambda: V.reciprocal(out=t[:, 0:T], in_=t[:, 0:T]), reads=[tb_], writes=[tb_])
            return t, tb_

        def rms_rstd():
            def mk(c):
                def f():
                    s_, sb_ = sqt()
                    sy.op("act", lambda: A.activation(out=s_, in_=x[:, c, :], func=AF.Square),
                          reads=[xb[c]], writes=[sb_])
                    return s_, sb_
                return f
            bk, bkb = colsum([mk(c) for c in range(KC)])
            return rstd_from(bk, bkb, D)

        def norm_to_h(gname):
            r, rb = rms_rstd()
            for c in range(KC):
                sy.op("dve", lambda c=c: V.scalar_tensor_tensor(
                    out=h[:, c, :], in0=x[:, c, :], scalar=P(gname, c), in1=r[:, 0:T],
                    op0=ALU.mult, op1=ALU.mult), reads=[xb[c], rb, pvb], writes=[hb[c]])

        def conv_taps(acc, accb, ub, ubb, K, wname, wstride, widx, bias=None):
            wcol = lambda k: P(wname, k * wstride + widx)
            if bias is None:
                sy.op("dve", lambda: V.tensor_scalar(out=acc[:, 0:T], in0=ub[:, K - 1:K - 1 + T], scalar1=wcol(K - 1),
                                                     scalar2=None, op0=ALU.mult), reads=[ubb, pvb], writes=[accb])
            else:
                sy.op("dve", lambda: V.tensor_scalar(out=acc[:, 0:T], in0=ub[:, K - 1:K - 1 + T], scalar1=wcol(K - 1),
                                                     scalar2=bias, op0=ALU.mult, op1=ALU.add),
                      reads=[ubb, pvb], writes=[accb])
            for k in range(K - 2, -1, -1):
                sy.op("dve", lambda k=k: V.scalar_tensor_tensor(
                    out=acc[:, 0:T], in0=ub[:, k:k + T], scalar=wcol(k), in1=acc[:, 0:T],
                    op0=ALU.mult, op1=ALU.add), reads=[ubb, accb, pvb], writes=[accb])

        def history(ub, ubb, stash_ap, H):
            sy.op("dve", lambda: V.tensor_copy(out=ub[:, 0:H], in_=stash_ap), reads=[stb], writes=[ubb])
            sy.op("dve", lambda: V.tensor_copy(out=stash_ap, in_=ub[:, T:T + H]), reads=[ubb], writes=[stb])

        def mask_x():
            if H == 0:
                return
            for c in range(KC):
                sy.op("dve", lambda c=c: V.tensor_tensor(out=x[:, c, 0:H], in0=x[:, c, 0:H], in1=maskw[:, 0:H], op=ALU.mult),
                      reads=[xb[c], maskb], writes=[xb[c]])

        def ffn(l):
            norm_to_h(f"ffnn{l}")
            hr = [(h[:, c, :], hb[c]) for c in range(KC)]
            gi = 0
            for j0 in range(0, CF, GSZ):
                grp = list(range(j0, min(j0 + GSZ, CF)))
                gaps = []
                for idx, j in enumerate(grp):
                    accs = []
                    for part, kind in ((0, "fg"), (1, "fu")):
                        col = part * CF + j
                        bk, bkb = proj(kind, hr)
                        ub, ubb = tmpb()
                        sy.op("act", lambda: A.activation(out=ub[:, 2:2 + T], in_=bk, func=AF.Copy),
                              reads=[bkb], writes=[ubb])
                        history(ub, ubb, st_f[:, l * 2 * CF + col, :], 2)
                        acc, accb = tmpb()
                        conv_taps(acc, accb, ub, ubb, 3, f"fcw{l}", 2 * CF, col, bias=P(f"fcb{l}", col))
                        accs.append((acc, accb))
                    (ag, agb), (au, aub) = accs
                    sy.op("act", lambda: A.activation(out=ag[:, 0:T], in_=ag[:, 0:T], func=AF.Silu),
                          reads=[agb], writes=[agb])
                    gs = (gi % 2) * GSZ + idx
                    sy.op("dve", lambda: V.tensor_tensor(out=ya[:, gs, :], in0=ag[:, 0:T], in1=au[:, 0:T], op=ALU.mult),
                          reads=[agb, aub], writes=[yab[gs]])
                    gaps.append((ya[:, gs, :], yab[gs]))
                dus = [take_unit("fd") for _ in grp]
                for d in range(KC):
                    bk, bkb = bank()
                    mm_group(bk, bkb, [(dus[i][1][:, d, :], gaps[i][0]) for i in range(len(grp))],
                             [du[2] for du in dus] + [g_[1] for g_ in gaps])
                    sy.op("dve", lambda d=d, bk=bk: V.tensor_tensor(out=x[:, d, :], in0=x[:, d, :], in1=bk, op=ALU.add),
                          reads=[xb[d], bkb], writes=[xb[d]])
                release(dus[-1][0])
                gi += 1

        def hybrid(l):
            i = l // 2
            norm_to_h(f"mixn{l}")
            hr = [(h[:, c, :], hb[c]) for c in range(KC)]
            s1, s1b = ps[:, 6, 0:T], psb[6]
            s2, s2b = ps[:, 7, 0:T], psb[7]
            def part_b(c):
                bv, bvb = proj("bv", hr)
                bg, bgb = proj("bg", hr)
                t1, t1b = tmpb()
                t2, t2b = tmpb()
                sy.op("act", lambda: A.activation(out=t1[:, 0:T], in_=bg, func=AF.Tanh, scale=0.5),
                      reads=[bgb], writes=[t1b])
                sy.op("act", lambda: A.activation(out=t2[:, 0:T], in_=bv, func=AF.Identity, scale=0.5),
                      reads=[bvb], writes=[t2b])
                ub, ubb = tmpb()
                sy.op("dve", lambda: V.scalar_tensor_tensor(out=ub[:, 30:30 + T], in0=t1[:, 0:T], scalar=1.0,
                                                            in1=t2[:, 0:T], op0=ALU.add, op1=ALU.mult),
                      reads=[t1b, t2b], writes=[ubb])
                history(ub, ubb, st_b[:, i * CB + c, :], 30)
                conv_taps(uc[:, c, :], ucb[c], ub, ubb, B_CONV_W, f"cbw{i}", CB, c, bias=P(f"cbb{i}", c))

            def stats_b(c):
                q1, q1b = sqt()
                q2, q2b = sqt()
                sy.op("act", lambda: A.activation(out=q1, in_=uc[:, c, :], func=AF.Copy), reads=[ucb[c]], writes=[q1b])
                sy.op("act", lambda: A.activation(out=q2, in_=uc[:, c, :], func=AF.Square), reads=[ucb[c]], writes=[q2b])
                sy.op("pe", lambda: PE.matmul(s1, ones[:, :], q1, start=(c == 0), stop=(c == CB - 1)),
                      reads=[q1b, onesb], writes=[s1b])
                sy.op("pe", lambda: PE.matmul(s2, ones[:, :], q2, start=(c == 0), stop=(c == CB - 1)),
                      reads=[q2b, onesb], writes=[s2b])
            def part_a(c):
                ax, axb = proj("ax", hr)
                ac, acb = proj("ac", hr)
                ab, abb = proj("ab", hr)
                t1, t1b = tmpb()
                sy.op("act", lambda: A.activation(out=t1[:, 0:T], in_=ax, func=AF.Copy), reads=[axb], writes=[t1b])
                ub, ubb = tmpb()
                sy.op("dve", lambda: V.tensor_tensor(out=ub[:, 2:2 + T], in0=ac, in1=t1[:, 0:T], op=ALU.mult),
                      reads=[acb, t1b], writes=[ubb])
                history(ub, ubb, st_a[:, i * CA + c, :], 2)
                acc, accb = tmpb()
                conv_taps(acc, accb, ub, ubb, A_CONV_W, f"caw{i}", CA, c)
                sy.op("dve", lambda: V.tensor_tensor(out=ya[:, c, :], in0=acc[:, 0:T], in1=ab, op=ALU.mult),
                      reads=[accb, abb], writes=[yab[c]])
            for c in range(max(CA, CB)):
                if c < CB:
                    part_b(c)
                if c < CA:
                    part_a(c)
                if 1 <= c <= CB:
                    stats_b(c - 1)
            for c in range(max(max(CA, CB) - 1, 0), CB):
                stats_b(c)
            mu, mub = stat[:, 1, :], statb[1]
            sy.op("dve", lambda: V.tensor_scalar(out=mu[:, 0:T], in0=s1, scalar1=1.0 / cfg.DB, scalar2=None, op0=ALU.mult),
                  reads=[s1b], writes=[mub])
            var, varb = stat[:, 2, :], statb[2]
            sy.op("dve", lambda: V.tensor_tensor(out=var[:, 0:T], in0=mu[:, 0:T], in1=mu[:, 0:T], op=ALU.mult),
                  reads=[mub], writes=[varb])
            sy.op("dve", lambda: V.scalar_tensor_tensor(out=var[:, 0:T], in0=s2, scalar=1.0 / cfg.DB, in1=var[:, 0:T],
                                                        op0=ALU.mult, op1=ALU.subtract),
                  reads=[s2b, varb], writes=[varb])
            sy.op("dve", lambda: V.tensor_scalar(out=var[:, 0:T], in0=var[:, 0:T], scalar1=EPS, scalar2=None, op0=ALU.add),
                  reads=[varb], writes=[varb])
            sy.op("act", lambda: A.activation(out=var[:, 0:T], in_=var[:, 0:T], func=AF.Sqrt), reads=[varb], writes=[varb])
            sy.op("dve", lambda: V.reciprocal(out=var[:, 0:T], in_=var[:, 0:T]), reads=[varb], writes=[varb])
            sy.op("dve", lambda: V.scalar_tensor_tensor(out=mu[:, 0:T], in0=mu[:, 0:T], scalar=-1.0, in1=var[:, 0:T],
                                                        op0=ALU.mult, op1=ALU.mult),
                  reads=[mub, varb], writes=[mub])
            for c in range(CB):
                z, zb = tmpb()
                sy.op("dve", lambda: V.tensor_tensor(out=z[:, 0:T], in0=uc[:, c, :], in1=var[:, 0:T], op=ALU.mult),
                      reads=[ucb[c], varb], writes=[zb])
                sy.op("dve", lambda: V.tensor_tensor(out=z[:, 0:T], in0=z[:, 0:T], in1=mu[:, 0:T], op=ALU.add),
                      reads=[zb, mub], writes=[zb])
                sy.op("act", lambda: A.activation(out=h[:, c, :], in_=z[:, 0:T], func=AF.Silu,
                                                  bias=P(f"lnb{i}", c), scale=P(f"lng{i}", c)),
                      reads=[zb, pvb], writes=[hb[c]])
            yr = [(ya[:, c, :], yab[c]) for c in range(CA)] + [(h[:, c, :], hb[c]) for c in range(CB)]
            for d in range(KC):
                bk, bkb = proj("wo", yr)
                sy.op("dve", lambda bk=bk: V.tensor_tensor(out=x[:, d, :], in0=x[:, d, :], in1=bk, op=ALU.add),
                      reads=[xb[d], bkb], writes=[xb[d]])

        def poolmix(l):
            i = l // 2
            PK = cfg.PK
            r, rb = rms_rstd()
            E = T + 15
            for c in range(KC):
                g = c // PK
                win = POOL_WINDOWS[g]
                hf, hfb = tmpb()
                sy.op("dve", lambda: V.scalar_tensor_tensor(out=hf[:, 15:E], in0=x[:, c, :], scalar=P(f"mixn{l}", c),
                                                            in1=r[:, 0:T], op0=ALU.mult, op1=ALU.mult),
                      reads=[xb[c], rb, pvb], writes=[hfb])
                history(hf, hfb, st_p[:, i * KC + c, :], 15)
                s_, s_b = hf, hfb
                lo, off = 0, 1
                while off < win:
                    n_, n_b = tmpb()
                    lo2 = lo + off
                    sy.op("dve", lambda s_=s_, n_=n_, lo2=lo2, off=off: V.tensor_tensor(
                        out=n_[:, lo2:E], in0=s_[:, lo2:E], in1=s_[:, lo2 - off:E - off], op=ALU.add),
                        reads=[s_b], writes=[n_b])
                    s_, s_b, lo, off = n_, n_b, lo2, off * 2
                sy.op("dve", lambda: V.scalar_tensor_tensor(out=h[:, c, :], in0=s_[:, 15:E], scalar=1.0 / win,
                                                            in1=hf[:, 15:E], op0=ALU.mult, op1=ALU.subtract),
                      reads=[s_b, hfb], writes=[hb[c]])
                t16, t16b = tmpb()
                sy.op("dve", lambda: V.tensor_tensor(out=t16[:, 0:16], in0=s_[:, 15 + H:31 + H], in1=rcnt[:, g, :], op=ALU.mult),
                      reads=[s_b, stb], writes=[t16b])
                sy.op("dve", lambda: V.tensor_tensor(out=h[:, c, H:H + 16], in0=t16[:, 0:16], in1=hf[:, 15 + H:31 + H], op=ALU.subtract),
                      reads=[t16b, hfb, hb[c]], writes=[hb[c]])
            for g in range(4):
                rhs = [(h[:, g * PK + k, :], hb[g * PK + k]) for k in range(PK)]
                for dl in range(PK):
                    d = g * PK + dl
                    bk, bkb = proj("pl", rhs)
                    sy.op("dve", lambda bk=bk: V.scalar_tensor_tensor(out=x[:, d, :], in0=bk, scalar=P(f"psc{i}", d),
                                                                      in1=x[:, d, :], op0=ALU.mult, op1=ALU.add),
                          reads=[bkb, xb[d], pvb], writes=[xb[d]])

        engs = [ET.SP, ET.Activation, ET.DVE, ET.PE]
        with nc.Fori(0, cfg.NT, engines=engs) as it:
            sy.dma("sp", xsem, x[:, :, :], xv[:, :, bass.ds(it * T, T)], writes=xb)
            sy.dma("sp", possem, posw[:, :], posd[:, bass.ds(it * T, H + 16)], writes=[posb])
            prefetch(NSLOT)
            if H > 0:
                sy.op("dve", lambda: V.tensor_scalar(out=maskw[:, 0:H], in0=posw[:, 0:H], scalar1=0.0, scalar2=None,
                                                     op0=ALU.is_ge), reads=[posb], writes=[maskb])
            for g, win in enumerate(POOL_WINDOWS):
                sy.op("dve", lambda g=g: V.tensor_scalar(out=rcnt[:, g, :], in0=posw[:, H:H + 16], scalar1=1.0, scalar2=1.0,
                                                         op0=ALU.add, op1=ALU.max), reads=[posb], writes=[stb])
                sy.op("dve", lambda g=g, win=win: V.tensor_scalar(out=rcnt[:, g, :], in0=rcnt[:, g, :], scalar1=float(win),
                                                                  scalar2=None, op0=ALU.min), reads=[stb], writes=[stb])
                sy.op("dve", lambda g=g: V.reciprocal(out=rcnt[:, g, :], in_=rcnt[:, g, :]), reads=[stb], writes=[stb])
            for l in range(cfg.DEPTH):
                if l % 2 == 0:
                    hybrid(l)
                else:
                    poolmix(l)
                mask_x()
                ffn(l)
                mask_x()
            assert state["next_use"] == NU and state["next_load"] == NU
            r, rb = rms_rstd()
            for c in range(KC):
                sy.op("dve", lambda c=c: V.scalar_tensor_tensor(
                    out=x[:, c, :], in0=x[:, c, :], scalar=P("finn", c), in1=r[:, 0:T],
                    op0=ALU.mult, op1=ALU.mult), reads=[xb[c], rb, pvb], writes=[xb[c]])
            sy.dma("sp", osem, ov[:, :, bass.ds(it * T, T)], x[:, :, :], reads=xb)
            sy.wait_all("sp")
            sy.clear_all("sp")
            nc.sync.sem_inc(barB, 1)
            for e in ("pe", "act", "dve"):
                sy.eng[e].wait_ge(barB, it + 2)
    return nc


FULL = dict(D=4096, DA=2048, DB=2048, DFF=11008, DEPTH=4, S=16384, T=360, NCORES=8)


def run(cfg, inp, trace=False):
    n, H, SC = cfg.NCORES, cfg.H, cfg.SC
    xs = np.asarray(inp["x"], dtype=np.float32).reshape(cfg.S, cfg.D)
    width = (n - 1) * SC + cfg.SP
    xfull = np.zeros((cfg.D, width), np.float32)
    xfull[:, H:H + cfg.S] = xs.T
    wall, rows = pack_weights(cfg, inp)
    pk = pack_params(cfg, inp)
    pvn = pk.build()
    nc = build_program(cfg, pvn.shape[1], pk.off, rows, wall.shape[0])
    in_maps = []
    for c in range(n):
        pos = (c * SC - H + np.arange(cfg.SP)).astype(np.float32)
        in_maps.append({"xT": np.ascontiguousarray(xfull[:, c * SC:c * SC + cfg.SP]), "wall": wall, "pv": pvn,
                        "pos": np.ascontiguousarray(np.broadcast_to(pos[None, :], (128, cfg.SP)))})
    res = run_bass_kernel_spmd(nc, in_maps, core_ids=list(range(n)), trace=trace)
    out = np.empty((cfg.S, cfg.D), np.float32)
    for c in range(n):
        oT = np.asarray(res.results[c]["outT"])
        out[c * SC:(c + 1) * SC, :] = oT[:, H:H + SC].T
    return out.reshape(1, cfg.S, cfg.D), res


def kernel(**inputs):
    cfg = Cfg(**FULL)
    out, _ = run(cfg, inputs)
    return out
```

```python
import contextlib
import numpy as np
import concourse.bass as bass
import concourse.mybir as mybir
from concourse.bass_utils import run_bass_kernel_spmd

F32 = mybir.dt.float32
BF16 = mybir.dt.bfloat16
AF = mybir.ActivationFunctionType
ALU = mybir.AluOpType
ET = mybir.EngineType

EPS = 1e-6
POOL_WINDOWS = (2, 4, 8, 16)
A_CONV_W = 3
B_CONV_W = 31
FFN_CONV_W = 3
GSZ = 4
NSLOT = 6
NTMP = 7
HIST = 32


class Cfg:
    def __init__(self, D, DA, DB, DFF, DEPTH, S, T, NCORES=1):
        self.D, self.DA, self.DB, self.DFF, self.DEPTH, self.S, self.T = D, DA, DB, DFF, DEPTH, S, T
        self.NCORES = NCORES
        self.KC, self.CA, self.CB, self.CF = D // 128, DA // 128, DB // 128, DFF // 128
        assert D % 512 == 0 and DA % 128 == 0 and DB % 128 == 0 and DFF % 128 == 0 and DA + DB == D
        assert S % NCORES == 0
        self.SC = S // NCORES
        self.H = 0 if NCORES == 1 else sum((B_CONV_W - 1 if l % 2 == 0 else POOL_WINDOWS[-1] - 1)
                                           + FFN_CONV_W - 1 for l in range(DEPTH))
        assert self.H + 16 <= T
        self.NT = -(-(self.H + self.SC) // T)
        self.SP = self.NT * T
        self.NE = (DEPTH + 1) // 2
        self.NO = DEPTH // 2
        self.PK = self.KC // 4


def unit_plan(cfg):
    units = []
    for l in range(cfg.DEPTH):
        i = l // 2
        if l % 2 == 0:
            for c in range(max(cfg.CA, cfg.CB)):
                if c < cfg.CB:
                    units.append(("bv", i, c, 0, cfg.KC))
                    units.append(("bg", i, c, 0, cfg.KC))
                if c < cfg.CA:
                    units.append(("ax", i, c, 0, cfg.KC))
                    units.append(("ac", i, c, 0, cfg.KC))
                    units.append(("ab", i, c, 0, cfg.KC))
            for d in range(cfg.KC):
                units.append(("wo", i, d, 0, cfg.KC))
        else:
            for g in range(4):
                for dl in range(cfg.PK):
                    units.append(("pl", i, g, dl, cfg.PK))
        for j0 in range(0, cfg.CF, GSZ):
            grp = list(range(j0, min(j0 + GSZ, cfg.CF)))
            for j in grp:
                units.append(("fg", l, j, 0, cfg.KC))
                units.append(("fu", l, j, 0, cfg.KC))
            for j in grp:
                units.append(("fd", l, j, 0, cfg.KC))
    return units


def _colunit(W, f0):
    k = W.shape[0] // 128
    return W[:, f0:f0 + 128].reshape(k, 128, 128).transpose(1, 0, 2)


def pack_weights(cfg, inp):
    units = unit_plan(cfg)
    rows = [0]
    for u in units:
        rows.append(rows[-1] + 128 * u[4])
    nrow = rows[-1]
    CVB = 128 * 64
    nrow_pad = -(-nrow // CVB) * CVB
    wall = np.zeros((nrow_pad, 128), np.float32)
    DA, DB, DFF = cfg.DA, cfg.DB, cfg.DFF
    for u, r0 in zip(units, rows[:-1]):
        kind, li, a, b, nk = u
        if kind in ("ax", "ac", "ab", "bv", "bg"):
            base = {"ax": 0, "ac": DA, "ab": 2 * DA, "bv": 3 * DA, "bg": 3 * DA + DB}[kind]
            blk = _colunit(inp["hyb_w_in"][li], base + a * 128)
        elif kind == "wo":
            blk = _colunit(inp["hyb_w_out"][li], a * 128)
        elif kind == "pl":
            blk = _colunit(inp["pool_w"][li][a], b * 128)
        elif kind == "fg":
            blk = _colunit(inp["ffn_w_up"][li], a * 128)
        elif kind == "fu":
            blk = _colunit(inp["ffn_w_up"][li], DFF + a * 128)
        elif kind == "fd":
            blk = inp["ffn_w_down"][li][a * 128:(a + 1) * 128, :].reshape(128, cfg.KC, 128)
        wall[r0:r0 + 128 * nk, :] = blk.reshape(128 * nk, 128)
    return wall, rows


class ParamPack:
    def __init__(self):
        self.cols = []
        self.off = {}
        self.n = 0

    def add(self, name, arr):
        arr = np.ascontiguousarray(arr, dtype=np.float32).reshape(128, -1)
        self.off[name] = self.n
        self.n += arr.shape[1]
        self.cols.append(arr)

    def build(self):
        return np.concatenate(self.cols, axis=1)


def _vec(v):
    return np.asarray(v).reshape(-1, 128).T


def _taps(w):
    K, C = w.shape
    return np.asarray(w).reshape(K, C // 128, 128).transpose(2, 0, 1).reshape(128, -1)


def pack_params(cfg, inp, with_data=True):
    pk = ParamPack()
    z = lambda *s: np.zeros(s, np.float32)
    for l in range(cfg.DEPTH):
        pk.add(f"mixn{l}", _vec(inp["mix_norm"][l]) if with_data else z(128, cfg.KC))
        pk.add(f"ffnn{l}", _vec(inp["ffn_norm"][l]) if with_data else z(128, cfg.KC))
        pk.add(f"fcw{l}", _taps(inp["ffn_conv"][l]) if with_data else z(128, 3 * 2 * cfg.CF))
        pk.add(f"fcb{l}", _vec(inp["ffn_conv_bias"][l]) if with_data else z(128, 2 * cfg.CF))
    pk.add("finn", _vec(inp["final_norm"]) if with_data else z(128, cfg.KC))
    for i in range(cfg.NE):
        pk.add(f"caw{i}", _taps(inp["hyb_conv_a"][i]) if with_data else z(128, A_CONV_W * cfg.CA))
        pk.add(f"cbw{i}", _taps(inp["hyb_conv_b"][i]) if with_data else z(128, B_CONV_W * cfg.CB))
        pk.add(f"cbb{i}", _vec(inp["hyb_conv_b_bias"][i]) if with_data else z(128, cfg.CB))
        pk.add(f"lng{i}", _vec(inp["hyb_ln_g"][i]) if with_data else z(128, cfg.CB))
        pk.add(f"lnb{i}", _vec(inp["hyb_ln_b"][i]) if with_data else z(128, cfg.CB))
    for i in range(cfg.NO):
        pk.add(f"psc{i}", _vec(inp["pool_scale"][i]) if with_data else z(128, cfg.KC))
    pk.add("pos16", np.tile(np.arange(1, 17, dtype=np.float32)[None, :], (128, 1)))
    return pk


class Buf:
    __slots__ = ("w", "r")

    def __init__(self):
        self.w = None
        self.r = {}


class Sync:
    def __init__(self, nc, stack):
        self.nc = nc
        self.eng = {"pe": nc.tensor, "act": nc.scalar, "dve": nc.vector, "sp": nc.sync, "pool": nc.gpsimd}
        self.semh = {}
        self.cnt = {}
        for k in self.eng:
            self.semh[k] = stack.enter_context(nc.semaphore("i_" + k))
            self.cnt[k] = 0
        self.stack = stack
        self.seen = {k: {} for k in self.eng}
        self.bufs = []

    def buf(self):
        b = Buf()
        self.bufs.append(b)
        return b

    def dma_sem(self, name):
        self.semh[name] = self.stack.enter_context(self.nc.semaphore("d_" + name))
        self.cnt[name] = 0
        return name

    def _need(self, e, reads, writes):
        ev = []
        for b in reads:
            if b.w is not None:
                ev.append(b.w)
        for b in writes:
            if b.w is not None and b.w[2] != e:
                ev.append(b.w)
            for r in b.r.values():
                if r[2] != e:
                    ev.append(r)
        return ev

    def _wait(self, e, events):
        best = {}
        for (sk, v, _pe) in events:
            if v > best.get(sk, 0):
                best[sk] = v
        for sk, v in best.items():
            if self.seen[e].get(sk, 0) >= v:
                continue
            self.eng[e].wait_ge(self.semh[sk], v)
            self.seen[e][sk] = v

    def _record(self, ev, ekey, reads, writes):
        for b in reads:
            b.r[ekey] = ev
        for b in writes:
            b.w = ev
            b.r = {}

    def op(self, e, emit, reads=(), writes=()):
        self._wait(e, self._need(e, reads, writes))
        ins = emit()
        self.cnt[e] += 1
        ins.then_inc(self.semh[e], 1)
        self._record((e, self.cnt[e], e), e, reads, writes)

    def dma(self, q, sem, out, in_, reads=(), writes=()):
        self._wait(q, self._need(None, reads, writes))
        self.cnt[sem] += 16
        self.eng[q].dma_start(out=out, in_=in_).then_inc(self.semh[sem], 16)
        self._record((sem, self.cnt[sem], "dma"), "dma:" + sem, reads, writes)

    def wait_all(self, e, skip=()):
        for sk, v in self.cnt.items():
            if v > 0 and sk not in skip and self.seen[e].get(sk, 0) < v:
                self.eng[e].wait_ge(self.semh[sk], v)
                self.seen[e][sk] = v

    def clear_all(self, e):
        for sk in self.semh:
            self.eng[e].sem_clear(self.semh[sk])
            self.cnt[sk] = 0
        for k in self.seen:
            self.seen[k] = {}
        for b in self.bufs:
            b.w = None
            b.r = {}


def build_program(cfg, npar, poff, unit_rows, nrow_pad):
    KC, CA, CB, CF, T, D = cfg.KC, cfg.CA, cfg.CB, cfg.CF, cfg.T, cfg.D
    units = unit_plan(cfg)
    NU = len(units)
    nc = bass.Bass("TRN2", target_bir_lowering=False)
    xT = nc.dram_tensor("xT", [D, cfg.SP], F32, kind="ExternalInput").ap()
    wall = nc.dram_tensor("wall", [nrow_pad, 128], F32, kind="ExternalInput").ap()
    pvd = nc.dram_tensor("pv", [128, npar], F32, kind="ExternalInput").ap()
    posd = nc.dram_tensor("pos", [128, cfg.SP], F32, kind="ExternalInput").ap()
    H = cfg.H
    outT = nc.dram_tensor("outT", [D, cfg.SP], F32, kind="ExternalOutput").ap()
    PR = 524288
    npage = -(-nrow_pad // PR)
    wbfs = [nc.dram_tensor(f"wbf{k}", [min(PR, nrow_pad - k * PR), 128], BF16).ap() for k in range(npage)]

    def wrows(r0, n):
        k = r0 // PR
        assert (r0 + n - 1) // PR == k, "weight block straddles a DRAM page"
        return wbfs[k][r0 - k * PR:r0 - k * PR + n, :]
    xv = xT.rearrange("(c p) t -> p c t", p=128)
    ov = outT.rearrange("(c p) t -> p c t", p=128)
    TW = T + HIST

    with contextlib.ExitStack() as st:
        sb = lambda name, shape, dt: st.enter_context(nc.sbuf_tensor(name, shape, dt))
        x = sb("x", [128, KC, T], F32)
        h = sb("h", [128, KC, T], BF16)
        wsl = sb("wsl", [128, NSLOT, KC * 128], BF16)
        NYA = max(CA, 2 * GSZ)
        ya = sb("ya", [128, NYA, T], BF16)
        uc = sb("uc", [128, max(CB, 1), T], F32)
        tmp = sb("tmp", [128, NTMP, TW], F32)
        sqb = sb("sqb", [128, 4, T], BF16)
        stat = sb("stat", [128, 3, T], F32)
        ones = sb("ones", [128, 128], BF16)
        pv = sb("pvs", [128, npar], F32)
        st_f = sb("st_f", [128, cfg.DEPTH * 2 * CF, 2], F32)
        st_a = sb("st_a", [128, max(cfg.NE * CA, 1), 2], F32)
        st_b = sb("st_b", [128, max(cfg.NE * CB, 1), 30], F32)
        st_p = sb("st_p", [128, max(cfg.NO * KC, 1), 15], F32)
        rcnt = sb("rcnt", [128, 4, 16], F32)
        posw = sb("posw", [128, H + 16], F32)
        maskw = sb("maskw", [128, max(H, 1)], F32)
        ps = st.enter_context(nc.psum_tensor("ps", [128, 8, 512], F32))
        barB = st.enter_context(nc.semaphore("barB"))
        NCG = 16
        cvs = [st.enter_context(nc.semaphore(f"cvs{g}")) for g in range(NCG)]

        sy = Sync(nc, st)
        xb = [sy.buf() for _ in range(KC)]
        hb = [sy.buf() for _ in range(KC)]
        wb = [sy.buf() for _ in range(NSLOT)]
        wsem = [sy.dma_sem(f"w{s}") for s in range(NSLOT)]
        yab = [sy.buf() for _ in range(NYA)]
        ucb = [sy.buf() for _ in range(max(CB, 1))]
        tb = [sy.buf() for _ in range(NTMP)]
        sqbb = [sy.buf() for _ in range(4)]
        statb = [sy.buf() for _ in range(3)]
        psb = [sy.buf() for _ in range(8)]
        stb = sy.buf()
        pvb = sy.buf()
        onesb = sy.buf()
        xsem = sy.dma_sem("xld")
        osem = sy.dma_sem("ost")
        psem = sy.dma_sem("pld")
        possem = sy.dma_sem("posld")
        posb = sy.buf()
        maskb = sy.buf()

        V, A, PE = nc.vector, nc.scalar, nc.tensor

        def P(name, c=0, n=1):
            o = poff[name] + c
            return pv[:, o:o + n]

        CVB = 128 * 64
        ncv = nrow_pad // CVB
        per_g = -(-ncv // NCG)
        cvn = [0] * NCG
        for k in range(ncv):
            src = wall[k * CVB:(k + 1) * CVB, :].rearrange("(p j) f -> p (j f)", p=128)
            dst = wrows(k * CVB, CVB).rearrange("(p j) f -> p (j f)", p=128)
            nc.gpsimd.dma_start(out=dst, in_=src).then_inc(cvs[k // per_g], 16)
            cvn[k // per_g] += 16
        cv_waited = set()
        sy.dma("sp", psem, pv[:, :], pvd, writes=[pvb])
        sy.op("dve", lambda: V.memset(ones[:, :], 1.0), writes=[onesb])
        sy.op("dve", lambda: V.memset(st_f[:, :, :], 0.0), writes=[stb])
        sy.op("dve", lambda: V.memset(st_a[:, :, :], 0.0), writes=[stb])
        sy.op("dve", lambda: V.memset(st_b[:, :, :], 0.0), writes=[stb])
        sy.op("dve", lambda: V.memset(st_p[:, :, :], 0.0), writes=[stb])
        sy.wait_all("sp")
        sy.clear_all("sp")
        nc.sync.sem_inc(barB, 1)
        for e in ("pe", "act", "dve"):
            sy.eng[e].wait_ge(barB, 1)

        state = {"bank": 0, "tmp": 0, "sq": 0, "next_load": 0, "next_use": 0, "nb": 8}

        def bank():
            b = state["bank"]
            state["bank"] = (b + 1) % state["nb"]
            return ps[:, b, 0:T], psb[b]

        def tmpb():
            k = state["tmp"]
            state["tmp"] = (k + 1) % NTMP
            return tmp[:, k, :], tb[k]

        def sqt():
            k = state["sq"]
            state["sq"] = (k + 1) % 4
            return sqb[:, k, :], sqbb[k]

        def load_unit(u):
            kind, li, a, b, nk = units[u]
            s = u % NSLOT
            r0 = unit_rows[u]
            src = wrows(r0, 128 * nk).rearrange("(p k) f -> p (k f)", p=128)
            for g in range((r0 // CVB) // per_g, ((r0 + 128 * nk - 1) // CVB) // per_g + 1):
                if g not in cv_waited:
                    cv_waited.add(g)
                    nc.sync.wait_ge(cvs[g], cvn[g])
            sy.dma("sp", wsem[s], wsl[:, s, 0:nk * 128], src, writes=[wb[s]])

        def prefetch(upto):
            while state["next_load"] < min(upto, NU):
                load_unit(state["next_load"])
                state["next_load"] += 1

        def take_unit(kind):
            u = state["next_use"]
            assert units[u][0] == kind, (units[u], kind)
            state["next_use"] += 1
            s = u % NSLOT
            return u, wsl[:, s, :].rearrange("p (k f) -> p k f", f=128), wb[s]

        def release(u_last):
            prefetch(u_last + 1 + NSLOT)

        def mm_group(bk, bkb, pairs, reads):
            def emit():
                ins = None
                n = len(pairs)
                for k, (l_, r_) in enumerate(pairs):
                    ins = PE.matmul(bk, l_, r_, start=(k == 0), stop=(k == n - 1))
                return ins
            sy.op("pe", emit, reads=reads, writes=[bkb])

        def proj(kind, rhs):
            u, wv, wbuf = take_unit(kind)
            bk, bkb = bank()
            mm_group(bk, bkb, [(wv[:, k, :], r[0]) for k, r in enumerate(rhs)],
                     [wbuf] + [r[1] for r in rhs])
            release(u)
            return bk, bkb

        def colsum(terms, scale_bias=None):
            bk, bkb = bank()
            n = len(terms)
            for k, mk in enumerate(terms):
                ap, b = mk()
                sy.op("pe", lambda ap=ap, k=k: PE.matmul(bk, ones[:, :], ap, start=(k == 0), stop=(k == n - 1)),
                      reads=[b, onesb], writes=[bkb])
            return bk, bkb

        def rstd_from(bk, bkb, n):
            t, tb_ = stat[:, 0, :], statb[0]
            sy.op("dve", lambda: V.tensor_scalar(out=t[:, 0:T], in0=bk, scalar1=1.0 / n, scalar2=EPS,
                                                 op0=ALU.mult, op1=ALU.add), reads=[bkb], writes=[tb_])
            sy.op("act", lambda: A.activation(out=t[:, 0:T], in_=t[:, 0:T], func=AF.Sqrt), reads=[tb_], writes=[tb_])
            sy.op("dve", lambda: V.reciprocal(out=t[:, 0:T], in_=t[:, 0:T]), reads=[tb_], writes=[tb_])
            return t, tb_

        def rms_rstd():
            def mk(c):
                def f():
                    s_, sb_ = sqt()
                    sy.op("act", lambda: A.activation(out=s_, in_=x[:, c, :], func=AF.Square),
                          reads=[xb[c]], writes=[sb_])
                    return s_, sb_
                return f
            bk, bkb = colsum([mk(c) for c in range(KC)])
            return rstd_from(bk, bkb, D)

        def norm_to_h(gname):
            r, rb = rms_rstd()
            for c in range(KC):
                sy.op("dve", lambda c=c: V.scalar_tensor_tensor(
                    out=h[:, c, :], in0=x[:, c, :], scalar=P(gname, c), in1=r[:, 0:T],
                    op0=ALU.mult, op1=ALU.mult), reads=[xb[c], rb, pvb], writes=[hb[c]])

        def conv_taps(acc, accb, ub, ubb, K, wname, wstride, widx, bias=None):
            wcol = lambda k: P(wname, k * wstride + widx)
            if bias is None:
                sy.op("dve", lambda: V.tensor_scalar(out=acc[:, 0:T], in0=ub[:, K - 1:K - 1 + T], scalar1=wcol(K - 1),
                                                     scalar2=None, op0=ALU.mult), reads=[ubb, pvb], writes=[accb])
            else:
                sy.op("dve", lambda: V.tensor_scalar(out=acc[:, 0:T], in0=ub[:, K - 1:K - 1 + T], scalar1=wcol(K - 1),
                                                     scalar2=bias, op0=ALU.mult, op1=ALU.add),
                      reads=[ubb, pvb], writes=[accb])
            for k in range(K - 2, -1, -1):
                sy.op("dve", lambda k=k: V.scalar_tensor_tensor(
                    out=acc[:, 0:T], in0=ub[:, k:k + T], scalar=wcol(k), in1=acc[:, 0:T],
                    op0=ALU.mult, op1=ALU.add), reads=[ubb, accb, pvb], writes=[accb])

        def history(ub, ubb, stash_ap, H):
            sy.op("dve", lambda: V.tensor_copy(out=ub[:, 0:H], in_=stash_ap), reads=[stb], writes=[ubb])
            sy.op("dve", lambda: V.tensor_copy(out=stash_ap, in_=ub[:, T:T + H]), reads=[ubb], writes=[stb])

        def mask_x():
            if H == 0:
                return
            for c in range(KC):
                sy.op("dve", lambda c=c: V.tensor_tensor(out=x[:, c, 0:H], in0=x[:, c, 0:H], in1=maskw[:, 0:H], op=ALU.mult),
                      reads=[xb[c], maskb], writes=[xb[c]])

        def ffn(l):
            norm_to_h(f"ffnn{l}")
            hr = [(h[:, c, :], hb[c]) for c in range(KC)]
            gi = 0
            for j0 in range(0, CF, GSZ):
                grp = list(range(j0, min(j0 + GSZ, CF)))
                gaps = []
                for idx, j in enumerate(grp):
                    accs = []
                    for part, kind in ((0, "fg"), (1, "fu")):
                        col = part * CF + j
                        bk, bkb = proj(kind, hr)
                        ub, ubb = tmpb()
                        sy.op("act", lambda: A.activation(out=ub[:, 2:2 + T], in_=bk, func=AF.Copy),
                              reads=[bkb], writes=[ubb])
                        history(ub, ubb, st_f[:, l * 2 * CF + col, :], 2)
                        acc, accb = tmpb()
                        conv_taps(acc, accb, ub, ubb, 3, f"fcw{l}", 2 * CF, col, bias=P(f"fcb{l}", col))
                        accs.append((acc, accb))
                    (ag, agb), (au, aub) = accs
                    sy.op("act", lambda: A.activation(out=ag[:, 0:T], in_=ag[:, 0:T], func=AF.Silu),
                          reads=[agb], writes=[agb])
                    gs = (gi % 2) * GSZ + idx
                    sy.op("dve", lambda: V.tensor_tensor(out=ya[:, gs, :], in0=ag[:, 0:T], in1=au[:, 0:T], op=ALU.mult),
                          reads=[agb, aub], writes=[yab[gs]])
                    gaps.append((ya[:, gs, :], yab[gs]))
                dus = [take_unit("fd") for _ in grp]
                for d in range(KC):
                    bk, bkb = bank()
                    mm_group(bk, bkb, [(dus[i][1][:, d, :], gaps[i][0]) for i in range(len(grp))],
                             [du[2] for du in dus] + [g_[1] for g_ in gaps])
                    sy.op("dve", lambda d=d, bk=bk: V.tensor_tensor(out=x[:, d, :], in0=x[:, d, :], in1=bk, op=ALU.add),
                          reads=[xb[d], bkb], writes=[xb[d]])
                release(dus[-1][0])
                gi += 1

        def hybrid(l):
            i = l // 2
            state["nb"] = 6
            state["bank"] = 0
            norm_to_h(f"mixn{l}")
            hr = [(h[:, c, :], hb[c]) for c in range(KC)]
            s1, s1b = ps[:, 6, 0:T], psb[6]
            s2, s2b = ps[:, 7, 0:T], psb[7]
            def part_b(c):
                bv, bvb = proj("bv", hr)
                bg, bgb = proj("bg", hr)
                t1, t1b = tmpb()
                t2, t2b = tmpb()
                sy.op("act", lambda: A.activation(out=t1[:, 0:T], in_=bg, func=AF.Tanh, scale=0.5),
                      reads=[bgb], writes=[t1b])
                sy.op("act", lambda: A.activation(out=t2[:, 0:T], in_=bv, func=AF.Identity, scale=0.5),
                      reads=[bvb], writes=[t2b])
                ub, ubb = tmpb()
                sy.op("dve", lambda: V.scalar_tensor_tensor(out=ub[:, 30:30 + T], in0=t1[:, 0:T], scalar=1.0,
                                                            in1=t2[:, 0:T], op0=ALU.add, op1=ALU.mult),
                      reads=[t1b, t2b], writes=[ubb])
                history(ub, ubb, st_b[:, i * CB + c, :], 30)
                conv_taps(uc[:, c, :], ucb[c], ub, ubb, B_CONV_W, f"cbw{i}", CB, c, bias=P(f"cbb{i}", c))
                q1, q1b = sqt()
                q2, q2b = sqt()
                sy.op("act", lambda: A.activation(out=q1, in_=uc[:, c, :], func=AF.Copy), reads=[ucb[c]], writes=[q1b])
                sy.op("act", lambda: A.activation(out=q2, in_=uc[:, c, :], func=AF.Square), reads=[ucb[c]], writes=[q2b])
                sy.op("pe", lambda: PE.matmul(s1, ones[:, :], q1, start=(c == 0), stop=(c == CB - 1)),
                      reads=[q1b, onesb], writes=[s1b])
                sy.op("pe", lambda: PE.matmul(s2, ones[:, :], q2, start=(c == 0), stop=(c == CB - 1)),
                      reads=[q2b, onesb], writes=[s2b])
            def part_a(c):
                ax, axb = proj("ax", hr)
                ac, acb = proj("ac", hr)
                ab, abb = proj("ab", hr)
                t1, t1b = tmpb()
                sy.op("act", lambda: A.activation(out=t1[:, 0:T], in_=ax, func=AF.Copy), reads=[axb], writes=[t1b])
                ub, ubb = tmpb()
                sy.op("dve", lambda: V.tensor_tensor(out=ub[:, 2:2 + T], in0=ac, in1=t1[:, 0:T], op=ALU.mult),
                      reads=[acb, t1b], writes=[ubb])
                history(ub, ubb, st_a[:, i * CA + c, :], 2)
                acc, accb = tmpb()
                conv_taps(acc, accb, ub, ubb, A_CONV_W, f"caw{i}", CA, c)
                sy.op("dve", lambda: V.tensor_tensor(out=ya[:, c, :], in0=acc[:, 0:T], in1=ab, op=ALU.mult),
                      reads=[accb, abb], writes=[yab[c]])
            for c in range(max(CA, CB)):
                if c < CB:
                    part_b(c)
                if c < CA:
                    part_a(c)
            mu, mub = stat[:, 1, :], statb[1]
            sy.op("dve", lambda: V.tensor_scalar(out=mu[:, 0:T], in0=s1, scalar1=1.0 / cfg.DB, scalar2=None, op0=ALU.mult),
                  reads=[s1b], writes=[mub])
            var, varb = stat[:, 2, :], statb[2]
            sy.op("dve", lambda: V.tensor_tensor(out=var[:, 0:T], in0=mu[:, 0:T], in1=mu[:, 0:T], op=ALU.mult),
                  reads=[mub], writes=[varb])
            sy.op("dve", lambda: V.scalar_tensor_tensor(out=var[:, 0:T], in0=s2, scalar=1.0 / cfg.DB, in1=var[:, 0:T],
                                                        op0=ALU.mult, op1=ALU.subtract),
                  reads=[s2b, varb], writes=[varb])
            sy.op("dve", lambda: V.tensor_scalar(out=var[:, 0:T], in0=var[:, 0:T], scalar1=EPS, scalar2=None, op0=ALU.add),
                  reads=[varb], writes=[varb])
            sy.op("act", lambda: A.activation(out=var[:, 0:T], in_=var[:, 0:T], func=AF.Sqrt), reads=[varb], writes=[varb])
            sy.op("dve", lambda: V.reciprocal(out=var[:, 0:T], in_=var[:, 0:T]), reads=[varb], writes=[varb])
            sy.op("dve", lambda: V.scalar_tensor_tensor(out=mu[:, 0:T], in0=mu[:, 0:T], scalar=-1.0, in1=var[:, 0:T],
                                                        op0=ALU.mult, op1=ALU.mult),
                  reads=[mub, varb], writes=[mub])
            for c in range(CB):
                z, zb = tmpb()
                sy.op("dve", lambda: V.tensor_tensor(out=z[:, 0:T], in0=uc[:, c, :], in1=var[:, 0:T], op=ALU.mult),
                      reads=[ucb[c], varb], writes=[zb])
                sy.op("dve", lambda: V.tensor_tensor(out=z[:, 0:T], in0=z[:, 0:T], in1=mu[:, 0:T], op=ALU.add),
                      reads=[zb, mub], writes=[zb])
                sy.op("act", lambda: A.activation(out=h[:, c, :], in_=z[:, 0:T], func=AF.Silu,
                                                  bias=P(f"lnb{i}", c), scale=P(f"lng{i}", c)),
                      reads=[zb, pvb], writes=[hb[c]])
            yr = [(ya[:, c, :], yab[c]) for c in range(CA)] + [(h[:, c, :], hb[c]) for c in range(CB)]
            for d in range(KC):
                bk, bkb = proj("wo", yr)
                sy.op("dve", lambda bk=bk: V.tensor_tensor(out=x[:, d, :], in0=x[:, d, :], in1=bk, op=ALU.add),
                      reads=[xb[d], bkb], writes=[xb[d]])
            state["nb"] = 8

        def poolmix(l):
            i = l // 2
            PK = cfg.PK
            r, rb = rms_rstd()
            E = T + 15
            for c in range(KC):
                g = c // PK
                win = POOL_WINDOWS[g]
                hf, hfb = tmpb()
                sy.op("dve", lambda: V.scalar_tensor_tensor(out=hf[:, 15:E], in0=x[:, c, :], scalar=P(f"mixn{l}", c),
                                                            in1=r[:, 0:T], op0=ALU.mult, op1=ALU.mult),
                      reads=[xb[c], rb, pvb], writes=[hfb])
                history(hf, hfb, st_p[:, i * KC + c, :], 15)
                s_, s_b = hf, hfb
                lo, off = 0, 1
                while off < win:
                    n_, n_b = tmpb()
                    lo2 = lo + off
                    sy.op("dve", lambda s_=s_, n_=n_, lo2=lo2, off=off: V.tensor_tensor(
                        out=n_[:, lo2:E], in0=s_[:, lo2:E], in1=s_[:, lo2 - off:E - off], op=ALU.add),
                        reads=[s_b], writes=[n_b])
                    s_, s_b, lo, off = n_, n_b, lo2, off * 2
                sy.op("dve", lambda: V.scalar_tensor_tensor(out=h[:, c, :], in0=s_[:, 15:E], scalar=1.0 / win,
                                                            in1=hf[:, 15:E], op0=ALU.mult, op1=ALU.subtract),
                      reads=[s_b, hfb], writes=[hb[c]])
                t16, t16b = tmpb()
                sy.op("dve", lambda: V.tensor_tensor(out=t16[:, 0:16], in0=s_[:, 15 + H:31 + H], in1=rcnt[:, g, :], op=ALU.mult),
                      reads=[s_b, stb], writes=[t16b])
                sy.op("dve", lambda: V.tensor_tensor(out=h[:, c, H:H + 16], in0=t16[:, 0:16], in1=hf[:, 15 + H:31 + H], op=ALU.subtract),
                      reads=[t16b, hfb, hb[c]], writes=[hb[c]])
            for g in range(4):
                rhs = [(h[:, g * PK + k, :], hb[g * PK + k]) for k in range(PK)]
                for dl in range(PK):
                    d = g * PK + dl
                    bk, bkb = proj("pl", rhs)
                    sy.op("dve", lambda bk=bk: V.scalar_tensor_tensor(out=x[:, d, :], in0=bk, scalar=P(f"psc{i}", d),
                                                                      in1=x[:, d, :], op0=ALU.mult, op1=ALU.add),
                          reads=[bkb, xb[d], pvb], writes=[xb[d]])

        engs = [ET.SP, ET.Activation, ET.DVE, ET.PE]
        with nc.Fori(0, cfg.NT, engines=engs) as it:
            sy.dma("sp", xsem, x[:, :, :], xv[:, :, bass.ds(it * T, T)], writes=xb)
            sy.dma("sp", possem, posw[:, :], posd[:, bass.ds(it * T, H + 16)], writes=[posb])
            prefetch(NSLOT)
            if H > 0:
                sy.op("dve", lambda: V.tensor_scalar(out=maskw[:, 0:H], in0=posw[:, 0:H], scalar1=0.0, scalar2=None,
                                                     op0=ALU.is_ge), reads=[posb], writes=[maskb])
            for g, win in enumerate(POOL_WINDOWS):
                sy.op("dve", lambda g=g: V.tensor_scalar(out=rcnt[:, g, :], in0=posw[:, H:H + 16], scalar1=1.0, scalar2=1.0,
                                                         op0=ALU.add, op1=ALU.max), reads=[posb], writes=[stb])
                sy.op("dve", lambda g=g, win=win: V.tensor_scalar(out=rcnt[:, g, :], in0=rcnt[:, g, :], scalar1=float(win),
                                                                  scalar2=None, op0=ALU.min), reads=[stb], writes=[stb])
                sy.op("dve", lambda g=g: V.reciprocal(out=rcnt[:, g, :], in_=rcnt[:, g, :]), reads=[stb], writes=[stb])
            for l in range(cfg.DEPTH):
                if l % 2 == 0:
                    hybrid(l)
                else:
                    poolmix(l)
                mask_x()
                ffn(l)
                mask_x()
            assert state["next_use"] == NU and state["next_load"] == NU
            r, rb = rms_rstd()
            for c in range(KC):
                sy.op("dve", lambda c=c: V.scalar_tensor_tensor(
                    out=x[:, c, :], in0=x[:, c, :], scalar=P("finn", c), in1=r[:, 0:T],
                    op0=ALU.mult, op1=ALU.mult), reads=[xb[c], rb, pvb], writes=[xb[c]])
            sy.dma("sp", osem, ov[:, :, bass.ds(it * T, T)], x[:, :, :], reads=xb)
            sy.wait_all("sp")
            sy.clear_all("sp")
            nc.sync.sem_inc(barB, 1)
            for e in ("pe", "act", "dve"):
                sy.eng[e].wait_ge(barB, it + 2)
    return nc


FULL = dict(D=4096, DA=2048, DB=2048, DFF=11008, DEPTH=4, S=16384, T=358, NCORES=8)


def run(cfg, inp, trace=False):
    n, H, SC = cfg.NCORES, cfg.H, cfg.SC
    xs = np.asarray(inp["x"], dtype=np.float32).reshape(cfg.S, cfg.D)
    width = (n - 1) * SC + cfg.SP
    xfull = np.zeros((cfg.D, width), np.float32)
    xfull[:, H:H + cfg.S] = xs.T
    wall, rows = pack_weights(cfg, inp)
    pk = pack_params(cfg, inp)
    pvn = pk.build()
    nc = build_program(cfg, pvn.shape[1], pk.off, rows, wall.shape[0])
    in_maps = []
    for c in range(n):
        pos = (c * SC - H + np.arange(cfg.SP)).astype(np.float32)
        in_maps.append({"xT": np.ascontiguousarray(xfull[:, c * SC:c * SC + cfg.SP]), "wall": wall, "pv": pvn,
                        "pos": np.ascontiguousarray(np.broadcast_to(pos[None, :], (128, cfg.SP)))})
    res = run_bass_kernel_spmd(nc, in_maps, core_ids=list(range(n)), trace=trace)
    out = np.empty((cfg.S, cfg.D), np.float32)
    for c in range(n):
        oT = np.asarray(res.results[c]["outT"])
        out[c * SC:(c + 1) * SC, :] = oT[:, H:H + SC].T
    return out.reshape(1, cfg.S, cfg.D), res


def kernel(**inputs):
    cfg = Cfg(**FULL)
    out, _ = run(cfg, inputs)
    return out
```
